# Optimizing a Trainium2 kernel written in Bass

```python
import math
import jax, jax.numpy as jnp
from jax import lax
import numpy as np

D_MODEL = 1024
BATCH = 4
SEQ = 8192
DEPTH = 2

CHUNK = 64
Q_BLOCK = 128
HEAD_DIM = 64
A_HEADS = 4
A_VDIM = 2 * HEAD_DIM
B_HEADS = 8
B_LEFT_CHUNKS = 8
REL_CLIP = 128
C_HEADS = 4
C_QK_DIM = 64
C_V_DIM = 128
D_CHUNK = 128
D_GROUPS = 4
D_WIDTH = 512
D_FF = -(-(8 * D_MODEL) // (3 * 256)) * 256

A_W = A_HEADS * A_VDIM
B_W = B_HEADS * HEAD_DIM
C_W = C_HEADS * C_V_DIM
C_QK_W = C_HEADS * C_QK_DIM
IN_EVEN = 3 * A_W + 3 * B_W
IN_ODD = 2 * C_QK_W + 2 * C_W + 2 * D_WIDTH
MIX_EVEN = A_W + B_W
MIX_ODD = C_W + D_WIDTH
N_EVEN = (DEPTH + 1) // 2
N_ODD = DEPTH // 2
ALPHA = (2.0 * DEPTH) ** 0.25
BETA = (8.0 * DEPTH) ** -0.25
LN_EPS = 1e-5
NEG_INF = -1e30

kernel_name = 'hybrid_streaming_diffattn_band_retnet_sgu'


def layer_norm(x, g, b):
    xf = x.astype(jnp.float32)
    mu = jnp.mean(xf, -1, keepdims=True)
    var = jnp.mean(jnp.square(xf - mu), -1, keepdims=True)
    return ((xf - mu) * lax.rsqrt(var + LN_EPS)).astype(x.dtype) * g + b


def rms_norm(x, g):
    xf = x.astype(jnp.float32)
    return (xf * lax.rsqrt(jnp.mean(xf * xf, -1, keepdims=True) + LN_EPS)).astype(x.dtype) * g


def group_norm_heads(x, g, b):
    xf = x.astype(jnp.float32)
    mu = jnp.mean(xf, -1, keepdims=True)
    var = jnp.mean(jnp.square(xf - mu), -1, keepdims=True)
    y = ((xf - mu) * lax.rsqrt(var + LN_EPS)).astype(x.dtype)
    return y.reshape(x.shape[0], x.shape[1], -1) * g + b


def alibi_slopes(n_heads):
    return 2.0 ** (-8.0 * jnp.arange(1, n_heads + 1, dtype=jnp.float32) / n_heads)


def diff_attention(q, k, v, lam, lam_init, subln_g):
    bsz, seq, h, _, d = q.shape
    nb = seq // Q_BLOCK
    kpos = jnp.arange(seq)
    slopes = alibi_slopes(h)
    scale = d ** -0.5

    def one_block(args):
        qb, i = args
        tq = i * Q_BLOCK + jnp.arange(Q_BLOCK)
        s = jnp.einsum('bqhmd,bkhmd->bhmqk', qb, k).astype(jnp.float32) * scale
        dist = jnp.abs(tq[:, None] - kpos[None, :]).astype(jnp.float32)
        allowed = (kpos[None, :] // CHUNK) <= (tq[:, None] // CHUNK)
        bias = jnp.where(allowed[None], -slopes[:, None, None] * dist[None], NEG_INF)
        p = jax.nn.softmax(s + bias[None, :, None], axis=-1)
        w = (p[:, :, 0] - lam * p[:, :, 1]).astype(v.dtype)
        return jnp.einsum('bhqk,bkhe->bqhe', w, v)

    qbs = q.reshape(bsz, nb, Q_BLOCK, h, 2, d).swapaxes(0, 1)
    o = lax.map(one_block, (qbs, jnp.arange(nb)))
    o = o.swapaxes(0, 1).reshape(bsz, seq, h, 2 * d)
    o = rms_norm(o, subln_g) * (1.0 - lam_init)
    return o.reshape(bsz, seq, h * 2 * d)


def band_attention(q, k, v, rel_bias):
    bsz, seq, h, d = q.shape
    nc = seq // CHUNK
    nband = B_LEFT_CHUNKS + 1
    pad = B_LEFT_CHUNKS * CHUNK
    band = nband * CHUNK
    rel = pad + jnp.arange(CHUNK)[:, None] - jnp.arange(band)[None, :]
    bias = rel_bias.astype(jnp.float32)[:, jnp.clip(rel, -REL_CLIP, REL_CLIP) + REL_CLIP]
    key_pos = (jnp.arange(nc)[:, None] - B_LEFT_CHUNKS) * CHUNK + jnp.arange(band)[None, :]
    valid = key_pos >= 0
    scale = d ** -0.5

    def one_seq(args):
        qs, ks, vs = args
        qc = qs.reshape(nc, CHUNK, h, d)
        kp = jnp.pad(ks, ((pad, 0), (0, 0), (0, 0))).reshape(nc + B_LEFT_CHUNKS, CHUNK, h, d)
        vp = jnp.pad(vs, ((pad, 0), (0, 0), (0, 0))).reshape(nc + B_LEFT_CHUNKS, CHUNK, h, d)
        kb = jnp.concatenate([kp[j:j + nc] for j in range(nband)], axis=1)
        vb = jnp.concatenate([vp[j:j + nc] for j in range(nband)], axis=1)
        s = jnp.einsum('cqhd,ckhd->hcqk', qc, kb).astype(jnp.float32) * scale + bias[:, None]
        s = jnp.where(valid[None, :, None, :], s, NEG_INF)
        p = jax.nn.softmax(s, axis=-1).astype(vs.dtype)
        return jnp.einsum('hcqk,ckhd->cqhd', p, vb).reshape(seq, h * d)

    return lax.map(one_seq, (q, k, v))


def retention(q, k, v):
    bsz, seq, h, dk = q.shape
    dv = v.shape[-1]
    nc = seq // CHUNK
    log_gamma = jnp.log(1.0 - 2.0 ** (-5.0 - jnp.arange(h, dtype=jnp.float32)))
    idx = jnp.arange(CHUNK, dtype=jnp.float32)
    diff = idx[:, None] - idx[None, :]
    decay_mask = jnp.where(diff >= 0, jnp.exp(log_gamma[:, None, None] * jnp.maximum(diff, 0.0)), 0.0)
    q_decay = jnp.exp(log_gamma[:, None] * (idx + 1.0))
    k_decay = jnp.exp(log_gamma[:, None] * (CHUNK - 1.0 - idx))
    chunk_decay = jnp.exp(log_gamma * CHUNK)

    def to_chunks(t):
        return t.astype(jnp.float32).reshape(bsz, nc, CHUNK, h, t.shape[-1]).transpose(1, 0, 3, 2, 4)

    qc, kc, vc = to_chunks(q), to_chunks(k * dk ** -0.5), to_chunks(v)
    scores = jnp.einsum('cbhld,cbhmd->cbhlm', qc, kc) * decay_mask
    inner = jnp.einsum('cbhlm,cbhme->cbhle', scores, vc)
    kv = jnp.einsum('cbhmd,hm,cbhme->cbhde', kc, k_decay, vc)

    def step(state, kv_c):
        return chunk_decay[None, :, None, None] * state + kv_c, state

    _, prev = lax.scan(step, jnp.zeros((bsz, h, dk, dv), jnp.float32), kv)
    cross = jnp.einsum('cbhld,cbhde->cbhle', qc, prev) * q_decay[:, :, None]
    o = (inner + cross).transpose(1, 0, 3, 2, 4).reshape(bsz, seq, h, dv)
    return o.astype(q.dtype)


def spatial_gating(z, w_s, b_s, ln_g, ln_b):
    bsz, seq, _ = z.shape
    u, v = jnp.split(z, 2, axis=-1)
    v = layer_norm(v, ln_g, ln_b)
    gw = D_WIDTH // D_GROUPS
    vc = v.reshape(bsz, seq // D_CHUNK, D_CHUNK, D_GROUPS, gw)
    causal = jnp.tril(jnp.ones((D_CHUNK, D_CHUNK), dtype=bool))
    ws = jnp.where(causal[None], w_s, 0)
    mixed = jnp.einsum('gts,bnsgc->bntgc', ws, vc) + b_s.T[:, :, None]
    return u * mixed.reshape(bsz, seq, D_WIDTH)


def swiglu(x, w_gate, w_up, w_down):
    return (jax.nn.silu(x @ w_gate) * (x @ w_up)) @ w_down


def setup_inputs(seed: int = 0) -> dict:
    key = jax.random.key(seed)
    ks = jax.random.split(key, 26)
    f32 = jnp.float32

    def nrm(k, shape, std):
        return jax.random.normal(k, shape, f32) * std

    ne, no = N_EVEN, N_ODD
    return {
        'x': nrm(ks[0], (BATCH, SEQ, D_MODEL), 1.0),
        'w_in_even': nrm(ks[1], (ne, D_MODEL, IN_EVEN), D_MODEL ** -0.5),
        'lam_q1': nrm(ks[2], (ne, HEAD_DIM), 0.1),
        'lam_k1': nrm(ks[3], (ne, HEAD_DIM), 0.1),
        'lam_q2': nrm(ks[4], (ne, HEAD_DIM), 0.1),
        'lam_k2': nrm(ks[5], (ne, HEAD_DIM), 0.1),
        'diff_subln_g': 1.0 + nrm(ks[6], (ne, A_VDIM), 0.02),
        'rel_bias': nrm(ks[7], (ne, B_HEADS, 2 * REL_CLIP + 1), 0.1),
        'w_out_even': nrm(ks[8], (ne, MIX_EVEN, D_MODEL), BETA * MIX_EVEN ** -0.5),
        'w_in_odd': nrm(ks[9], (no, D_MODEL, IN_ODD), D_MODEL ** -0.5),
        'ret_gn_g': 1.0 + nrm(ks[10], (no, C_W), 0.02),
        'ret_gn_b': nrm(ks[11], (no, C_W), 0.02),
        'sgu_ln_g': 1.0 + nrm(ks[12], (no, D_WIDTH), 0.02),
        'sgu_ln_b': nrm(ks[13], (no, D_WIDTH), 0.02),
        'sgu_w': nrm(ks[14], (no, D_GROUPS, D_CHUNK, D_CHUNK), D_CHUNK ** -0.5),
        'sgu_b': 1.0 + nrm(ks[15], (no, D_GROUPS, D_CHUNK), 0.02),
        'w_out_odd': nrm(ks[16], (no, MIX_ODD, D_MODEL), BETA * MIX_ODD ** -0.5),
        'ln_mix_g': 1.0 + nrm(ks[17], (DEPTH, D_MODEL), 0.02),
        'ln_mix_b': nrm(ks[18], (DEPTH, D_MODEL), 0.02),
        'ffn_w_gate': nrm(ks[19], (DEPTH, D_MODEL, D_FF), D_MODEL ** -0.5),
        'ffn_w_up': nrm(ks[20], (DEPTH, D_MODEL, D_FF), D_MODEL ** -0.5),
        'ffn_w_down': nrm(ks[21], (DEPTH, D_FF, D_MODEL), BETA * D_FF ** -0.5),
        'ln_ffn_g': 1.0 + nrm(ks[22], (DEPTH, D_MODEL), 0.02),
        'ln_ffn_b': nrm(ks[23], (DEPTH, D_MODEL), 0.02),
    }


def reference(x, w_in_even, lam_q1, lam_k1, lam_q2, lam_k2, diff_subln_g, rel_bias, w_out_even,
              w_in_odd, ret_gn_g, ret_gn_b, sgu_ln_g, sgu_ln_b, sgu_w, sgu_b, w_out_odd,
              ln_mix_g, ln_mix_b, ffn_w_gate, ffn_w_up, ffn_w_down, ln_ffn_g, ln_ffn_b):
    bsz, seq, _ = x.shape
    for l in range(DEPTH):
        j = l // 2
        if l % 2 == 0:
            hproj = x @ w_in_even[j]
            qa, ka, va, qb, kb, vb = jnp.split(
                hproj, [A_W, 2 * A_W, 3 * A_W, 3 * A_W + B_W, 3 * A_W + 2 * B_W], axis=-1)
            lam_init = 0.8 - 0.6 * math.exp(-0.3 * l)
            lam = (jnp.exp(jnp.sum(lam_q1[j].astype(jnp.float32) * lam_k1[j].astype(jnp.float32)))
                   - jnp.exp(jnp.sum(lam_q2[j].astype(jnp.float32) * lam_k2[j].astype(jnp.float32)))
                   + lam_init)
            o_a = diff_attention(qa.reshape(bsz, seq, A_HEADS, 2, HEAD_DIM),
                                 ka.reshape(bsz, seq, A_HEADS, 2, HEAD_DIM),
                                 va.reshape(bsz, seq, A_HEADS, A_VDIM),
                                 lam, lam_init, diff_subln_g[j])
            o_b = band_attention(qb.reshape(bsz, seq, B_HEADS, HEAD_DIM),
                                 kb.reshape(bsz, seq, B_HEADS, HEAD_DIM),
                                 vb.reshape(bsz, seq, B_HEADS, HEAD_DIM),
                                 rel_bias[j])
            mix = jnp.concatenate([o_a, o_b], axis=-1) @ w_out_even[j]
        else:
            hproj = x @ w_in_odd[j]
            qc, kc, vc, gc, zd = jnp.split(
                hproj, [C_QK_W, 2 * C_QK_W, 2 * C_QK_W + C_W, 2 * C_QK_W + 2 * C_W], axis=-1)
            o_c = retention(qc.reshape(bsz, seq, C_HEADS, C_QK_DIM),
                            kc.reshape(bsz, seq, C_HEADS, C_QK_DIM),
                            vc.reshape(bsz, seq, C_HEADS, C_V_DIM))
            o_c = group_norm_heads(o_c, ret_gn_g[j], ret_gn_b[j]) * jax.nn.silu(gc)
            o_d = spatial_gating(jax.nn.gelu(zd), sgu_w[j], sgu_b[j], sgu_ln_g[j], sgu_ln_b[j])
            mix = jnp.concatenate([o_c, o_d], axis=-1) @ w_out_odd[j]
        x = layer_norm(ALPHA * x + mix, ln_mix_g[l], ln_mix_b[l])
        x = layer_norm(ALPHA * x + swiglu(x, ffn_w_gate[l], ffn_w_up[l], ffn_w_down[l]),
                       ln_ffn_g[l], ln_ffn_b[l])
    return x
```

```python
import numpy as np
import concourse.bass as bass
import concourse.mybir as mybir

F32 = mybir.dt.float32
BF16 = mybir.dt.bfloat16
AF = mybir.ActivationFunctionType
ALU = mybir.AluOpType
AX = mybir.AxisListType


class Sem:
    __slots__ = ("h", "name")

    def __init__(self, h, name):
        self.h = h
        self.name = name


class Dep:
    __slots__ = ("name", "w", "r", "dsem", "dcnt", "multi", "wm")

    def __init__(self, name, multi=False):
        self.name = name
        self.multi = multi
        self.wm = {}
        self.w = None
        self.r = {}
        self.dsem = None
        self.dcnt = 0


class Eng:
    def __init__(self, nc, eng, name):
        self.eng = eng
        self.name = name
        self.sem = Sem(nc.alloc_semaphore("s_" + name), name)
        self.cnt = 0
        self.seen = {}


class KB:
    def __init__(self, nc):
        self.nc = nc
        self.pe = Eng(nc, nc.tensor, "pe")
        self.act = Eng(nc, nc.scalar, "act")
        self.dve = Eng(nc, nc.vector, "dve")
        self.pool = Eng(nc, nc.gpsimd, "pool")
        self.sp = Eng(nc, nc.sync, "sp")
        self.nsem = 5
        self.nwait = 0
        self.nins = 0
        self.dma_deps = []
        self.free_sems = []
        self.mark = None

    def phase_begin(self):
        self.mark = (self.nc.sbuf_base, self.nc.sbuf_top)

    def phase_end(self, mark=None):
        need = {}
        for d in self.dma_deps:
            if d.dcnt:
                self._add(need, (d.dsem, d.dcnt))
        self._waits(self.sp, need)
        self.nc.all_engine_barrier()
        for d in self.dma_deps:
            self.free_sems.append((d.dsem, d.dcnt))
            d.dsem = None
            d.dcnt = 0
        self.dma_deps = []
        mk = mark if mark is not None else self.mark
        if mk is not None:
            self.nc.sbuf_base, self.nc.sbuf_top = mk

    def _waits(self, E, need):
        for s, v in need.items():
            if E.seen.get(s, 0) >= v:
                continue
            E.eng.wait_ge(s.h, v)
            E.seen[s] = v
            self.nwait += 1

    @staticmethod
    def _add(need, sv):
        if sv is None:
            return
        s, v = sv
        if need.get(s, 0) < v:
            need[s] = v

    def _need(self, E, reads, writes, acc=False):
        need = {}
        for d in reads:
            self._add(need, d.w)
            for s, v in d.wm.items():
                self._add(need, (s, v))
        for d in writes:
            if not d.multi:
                if not (acc and d.w is not None and d.w[0] is E.sem):
                    self._add(need, d.w)
            for s, v in d.r.items():
                if acc and s is E.sem:
                    continue
                self._add(need, (s, v))
        return need

    def _record(self, sem, val, reads, writes):
        for d in reads:
            if d.r.get(sem, 0) < val:
                d.r[sem] = val
        for d in writes:
            if d.multi:
                d.wm[sem] = val
            else:
                d.w = (sem, val)
                d.r = {}

    def op(self, E, fn, reads=(), writes=(), acc=False):
        need = self._need(E, reads, writes, acc)
        self._waits(E, need)
        ins = fn()
        E.cnt += 1
        ins.then_inc(E.sem.h, 1)
        self.nins += 1
        self._record(E.sem, E.cnt, reads, writes)
        return ins

    def dma(self, Q, out, in_, reads=(), writes=(), sd=None, **kw):
        if sd.dsem is None:
            if self.free_sems:
                sd.dsem, sd.dcnt = self.free_sems.pop()
            else:
                sd.dsem = Sem(self.nc.alloc_semaphore("d_" + sd.name), sd.name)
                self.nsem += 1
            self.dma_deps.append(sd)
        need = self._need(Q, reads, writes)
        if sd.dcnt:
            self._add(need, (sd.dsem, sd.dcnt))
        self._waits(Q, need)
        ins = Q.eng.dma_start(out=out, in_=in_, **kw)
        sd.dcnt += 16
        ins.then_inc(sd.dsem.h, 16)
        self.nins += 1
        self._record(sd.dsem, sd.dcnt, reads, writes)
        return ins

    def collective(self, kind, groups, in_ap, out_ap, reads=(), writes=()):
        E = self.pool
        need = self._need(E, reads, writes)
        self._waits(E, need)
        sem = Sem(self.nc.alloc_semaphore("cc%d" % self.nsem), "cc")
        self.nsem += 1
        ins = self.nc.gpsimd.collective_compute(kind, ALU.bypass, replica_groups=groups, ins=[in_ap], outs=[out_ap])
        ins.then_inc(sem.h, 1)
        self.nins += 1
        self._record(sem, 1, reads, writes)
        return ins

    def wait_all(self, E, deps):
        need = {}
        for d in deps:
            self._add(need, d.w)
            for s, v in d.wm.items():
                self._add(need, (s, v))
            for s, v in d.r.items():
                self._add(need, (s, v))
        self._waits(E, need)


class Pool:
    def __init__(self, nc, name, shape, dtype, n):
        self.tiles = []
        for i in range(n):
            t = nc.alloc_sbuf_tensor(f"{name}{i}", list(shape), dtype).ap()
            self.tiles.append((t, Dep(f"{name}{i}")))
        self.i = 0

    def next(self):
        t = self.tiles[self.i % len(self.tiles)]
        self.i += 1
        return t


S = 8192
SLOPES = [2.0 ** (-8.0 * (h + 1) / 4) for h in range(4)]
NEG = -30000.0


class Ctx:
    def __init__(self, nc, kb, ident_dram):
        self.nc = nc
        self.kb = kb
        self.ps = []
        self.psd = []
        for i in range(8):
            self.ps.append(nc.alloc_psum_tensor(f"ps{i}", [128, 512], F32).ap())
            self.psd.append(Dep(f"ps{i}"))
        self.ident = nc.alloc_sbuf_tensor("ident_sb", [128, 128], BF16).ap()
        self.d_ident = Dep("ident")
        kb.dma(kb.sp, self.ident, ident_dram, writes=[self.d_ident], sd=self.d_ident)
        self.xpool = Pool(nc, "xin", [128, 1024], F32, 2)
        self.xbpool = Pool(nc, "xbf", [128, 1024], BF16, 2)
        self.tbank = 0

    def transpose_into(self, src_bf, d_src, dstT, d_dst, nchunk, tbanks, evac_eng=None):
        nc, kb = self.nc, self.kb
        bi = tbanks[self.tbank % len(tbanks)]
        self.tbank += 1
        psb = self.ps[bi].bitcast(BF16)
        dps = self.psd[bi]
        for c in range(nchunk):
            kb.op(kb.pe, lambda: nc.tensor.transpose(out=psb[:, c * 128:(c + 1) * 128], in_=src_bf[:, c * 128:(c + 1) * 128],
                                                     identity=self.ident),
                  reads=[d_src, self.d_ident], writes=[dps], acc=True)
        E = evac_eng or kb.dve
        if E is kb.dve:
            kb.op(E, lambda: nc.vector.tensor_copy(out=dstT, in_=psb[:, 0:nchunk * 128].rearrange("p (c n) -> p c n", c=nchunk)),
                  reads=[dps], writes=[d_dst], acc=True)
        else:
            kb.op(E, lambda: nc.scalar.copy(out=dstT, in_=psb[:, 0:nchunk * 128].rearrange("p (c n) -> p c n", c=nchunk)),
                  reads=[dps], writes=[d_dst], acc=True)

    def make_xT(self, x_rows, xT, d_xT, tbanks, d_xsrc=None, ntile=4):
        nc, kb = self.nc, self.kb
        for t in range(ntile):
            xt, dx = self.xpool.next()
            kb.dma(kb.sp, xt, x_rows[t * 128:(t + 1) * 128, :], reads=([d_xsrc] if d_xsrc else []), writes=[dx], sd=dx)
            xb, dxb = self.xbpool.next()
            kb.op(kb.act, lambda: nc.scalar.copy(out=xb, in_=xt), reads=[dx], writes=[dxb])
            self.transpose_into(xb, dxb, xT[:, :, t * 128:(t + 1) * 128], d_xT, 8, tbanks)


def load_w_bf16(nc, kb, w_dram, wt, d_wt, ncols, stage_pool, colchunk):
    wv = w_dram.rearrange("(c p) n -> p c n", p=128)
    for c0 in range(0, ncols, colchunk):
        c1 = min(ncols, c0 + colchunk)
        st, dst = stage_pool.next()
        kb.dma(kb.sp, st[:, :, 0:c1 - c0], wv[:, :, c0:c1], writes=[dst], sd=dst)
        kb.op(kb.pool, lambda: nc.gpsimd.tensor_copy(out=wt[:, :, c0:c1], in_=st[:, :, 0:c1 - c0]), reads=[dst], writes=[d_wt], acc=True)


def phase_A(nc, kb, ctx, D):
    wA = nc.alloc_sbuf_tensor("wA", [128, 8, 768], BF16).ap(); d_wA = Dep("wA")
    wst = Pool(nc, "wstA", [128, 8, 256], F32, 2)
    load_w_bf16(nc, kb, D["wa"], wA, d_wA, 768, wst, 256)
    xTp = Pool(nc, "xTA", [128, 8, 512], BF16, 2)
    fst = Pool(nc, "fstA", [128, 512], BF16, 3)
    vst = Pool(nc, "vstA", [128, 256], BF16, 2)
    d_QT = Dep("QT", multi=True); d_KT = Dep("KT", multi=True); d_V = Dep("V", multi=True)
    fmb = [2, 3]; vb = [4, 5]; fi = 0; vi = 0
    for g in range(S // 512):
        xT, d_xT = xTp.next()
        ctx.make_xT(D["xb"][g * 512:(g + 1) * 512, :], xT, d_xT, [0, 1])
        for ci in range(4):
            b = fmb[fi % 2]; fi += 1
            ps, dps = ctx.ps[b], ctx.psd[b]
            for k in range(8):
                kb.op(kb.pe, lambda: nc.tensor.matmul(ps, lhsT=wA[:, k, ci * 128:(ci + 1) * 128], rhs=xT[:, k, :], start=(k == 0), stop=(k == 7)),
                      reads=[d_wA, d_xT], writes=[dps], acc=True)
            st, dst = fst.next()
            sc = 0.125 if ci < 2 else 1.0
            kb.op(kb.act, lambda: nc.scalar.activation(out=st, in_=ps, func=AF.Copy, scale=sc), reads=[dps], writes=[dst])
            dstd, dd = (D["QT"], d_QT) if ci < 2 else (D["KT"], d_KT)
            kb.dma(kb.pool, dstd[ci % 2, :, g * 512:(g + 1) * 512], st, reads=[dst], writes=[dd], sd=dst)
        for t in range(4):
            b = vb[vi % 2]; vi += 1
            ps, dps = ctx.ps[b], ctx.psd[b]
            for k in range(8):
                kb.op(kb.pe, lambda: nc.tensor.matmul(ps[:, 0:256], lhsT=xT[:, k, t * 128:(t + 1) * 128], rhs=wA[:, k, 512:768], start=(k == 0), stop=(k == 7)),
                      reads=[d_wA, d_xT], writes=[dps], acc=True)
            st, dst = vst.next()
            kb.op(kb.dve, lambda: nc.vector.tensor_copy(out=st, in_=ps[:, 0:256]), reads=[dps], writes=[dst])
            kb.dma(kb.pool, D["V"][g * 512 + t * 128: g * 512 + (t + 1) * 128, :], st, reads=[dst], writes=[d_V], sd=dst)

    def ctile(name, shape, dt, src):
        t = nc.alloc_sbuf_tensor(name + "_sb", list(shape), dt).ap(); d = Dep(name)
        kb.dma(kb.sp, t, src, writes=[d], sd=d)
        return t, d
    cdiag, d_cdiag = ctile("cdiag", [128, 2, 128], F32, D["cdiag"].rearrange("h k q -> k h q"))
    offt, d_offt = ctile("offt", [128, 2, 67], F32, D["offtab"].rearrange("h p i -> p h i"))
    ones_b, d_ones_b = ctile("ones_b", [128, 128], BF16, D["ones_b"])
    ones_f, d_ones_f = ctile("ones_f", [128, 128], F32, D["ones_f"])
    gcol, d_gcol = ctile("gcol", [128, 1], F32, D["subg"])
    lamv, d_lamv = ctile("lamv", [128, 256], F32, D["lamv"].partition_broadcast(128).rearrange("p o n -> p (o n)"))
    lprod = nc.alloc_sbuf_tensor("lprod", [128, 128], F32).ap(); d_lprod = Dep("lprod")
    lsum = nc.alloc_sbuf_tensor("lsum", [128, 2], F32).ap(); d_lsum = Dep("lsum")
    lexp = nc.alloc_sbuf_tensor("lexp", [128, 2], F32).ap(); d_lexp = Dep("lexp")
    nlam = nc.alloc_sbuf_tensor("nlam", [128, 1], F32).ap(); d_nlam = Dep("nlam")
    epsc = nc.alloc_sbuf_tensor("epsc", [128, 1], F32).ap(); d_epsc = Dep("epsc")
    gsc = nc.alloc_sbuf_tensor("gsc", [128, 1], F32).ap(); d_gsc = Dep("gsc")
    lv = lamv.rearrange("p (a b n) -> p a b n", a=2, b=2)
    kb.op(kb.dve, lambda: nc.vector.tensor_tensor(out=lprod.rearrange("p (a n) -> p a n", a=2), in0=lv[:, :, 0, :], in1=lv[:, :, 1, :], op=ALU.mult),
          reads=[d_lamv], writes=[d_lprod])
    kb.op(kb.dve, lambda: nc.vector.tensor_reduce(out=lsum, in_=lprod.rearrange("p (a n) -> p a n", a=2), axis=AX.X, op=ALU.add),
          reads=[d_lprod], writes=[d_lsum])
    kb.op(kb.act, lambda: nc.scalar.activation(out=lexp, in_=lsum, func=AF.Exp), reads=[d_lsum], writes=[d_lexp])
    kb.op(kb.dve, lambda: nc.vector.tensor_tensor(out=nlam, in0=lexp[:, 1:2], in1=lexp[:, 0:1], op=ALU.subtract), reads=[d_lexp], writes=[d_nlam])
    kb.op(kb.dve, lambda: nc.vector.tensor_scalar(out=nlam, in0=nlam, scalar1=-0.2, scalar2=None, op0=ALU.add), reads=[d_nlam], writes=[d_nlam])
    kb.op(kb.dve, lambda: nc.vector.memset(epsc, 1e-5), writes=[d_epsc])
    kb.op(kb.dve, lambda: nc.vector.tensor_scalar(out=gsc, in0=gcol, scalar1=0.8, scalar2=None, op0=ALU.mult), reads=[d_gcol], writes=[d_gsc])

    KTm = []; Vh = []
    for h in range(2):
        row = []
        for m in range(2):
            t = nc.alloc_sbuf_tensor(f"KTs{h}{m}", [67, S], BF16).ap(); d = Dep(f"KT{h}{m}")
            kb.dma(kb.sp, t[0:64, :], D["KT"][h, m * 64:(m + 1) * 64, :], reads=[d_KT], writes=[d], sd=d)
            kb.dma(kb.sp, t[64:67, :], D["krows"][h], writes=[d], sd=d)
            row.append((t, d))
        KTm.append(row)
        t = nc.alloc_sbuf_tensor(f"Vh{h}", [128, S // 128, 128], BF16).ap(); d = Dep(f"Vh{h}")
        kb.dma(kb.sp, t, D["V"][:, h * 128:(h + 1) * 128].rearrange("(kb p) e -> p kb e", p=128), reads=[d_V], writes=[d], sd=d)
        Vh.append((t, d))
    Qp = [Pool(nc, f"Qm{m}", [67, 512], BF16, 2) for m in range(2)]
    pTp = [Pool(nc, f"pT{m}", [128, 512], BF16, 3) for m in range(2)]
    Sb = [[0, 1], [2, 3]]; sbi = [0, 0]
    Ob = [4, 5]; Zb = [6, 7]
    ep = {}
    for nm in ("rz0", "rz1", "o1", "o2", "oo", "sq", "rt", "rr"):
        ep[nm] = (nc.alloc_sbuf_tensor("ep_" + nm, [128, 512], F32).ap(), Dep("ep_" + nm))
    obp = Pool(nc, "oaout", [128, 512], BF16, 2)
    d_oa = D.get("_d_oa") or Dep("oa", multi=True)
    qrow_loaded = set()
    for h in range(2):
        V_t, d_V_t = Vh[h]
        for g in range(S // 512):
            Q = []
            for m in range(2):
                qt, dq = Qp[m].next()
                kb.dma(kb.sp, qt[0:64, :], D["QT"][h, m * 64:(m + 1) * 64, g * 512:(g + 1) * 512], reads=[d_QT], writes=[dq], sd=dq)
                kb.dma(kb.sp, qt[64:67, :], D["qrows"][h], writes=[dq], sd=dq)
                Q.append((qt, dq))
            nkb = 4 * g + 4

            def qk(kk):
                j = kk - 4 * g
                c0 = 128 * j if j > 0 else 0
                out = []
                for m in range(2):
                    b = Sb[m][sbi[m] % 2]; sbi[m] += 1
                    ps, dps = ctx.ps[b], ctx.psd[b]
                    kt, dk = KTm[h][m]
                    qt, dq = Q[m]
                    kb.op(kb.pe, lambda: nc.tensor.matmul(ps[:, c0:512], lhsT=kt[0:67, kk * 128:(kk + 1) * 128], rhs=qt[0:67, c0:512], start=True, stop=True),
                          reads=[dk, dq], writes=[dps])
                    if j >= 0:
                        kb.op(kb.dve, lambda: nc.vector.tensor_tensor(out=ps[:, c0:c0 + 128], in0=ps[:, c0:c0 + 128], in1=cdiag[:, h, :], op=ALU.add),
                              reads=[dps, d_cdiag], writes=[dps])
                    pT, dpT = pTp[m].next()
                    i = 4 * g - kk + 3
                    kb.op(kb.act, lambda: nc.scalar.activation(out=pT[:, c0:512], in_=ps[:, c0:512], func=AF.Exp, bias=offt[:, h, i:i + 1]),
                          reads=[dps, d_offt], writes=[dpT])
                    out.append((pT, dpT, c0))
                return out

            def pv(kk, res):
                for m in range(2):
                    pT, dpT, c0 = res[m]
                    kb.op(kb.pe, lambda: nc.tensor.matmul(ctx.ps[Ob[m]][:, c0:512], lhsT=V_t[:, kk, :], rhs=pT[:, c0:512], start=(kk == 0), stop=(kk == nkb - 1)),
                          reads=[d_V_t, dpT], writes=[ctx.psd[Ob[m]]], acc=True)
                    kb.op(kb.pe, lambda: nc.tensor.matmul(ctx.ps[Zb[m]][:, c0:512], lhsT=ones_b, rhs=pT[:, c0:512], start=(kk == 0), stop=(kk == nkb - 1)),
                          reads=[d_ones_b, dpT], writes=[ctx.psd[Zb[m]]], acc=True)

            r = qk(0)
            for kk in range(nkb):
                rn = qk(kk + 1) if kk + 1 < nkb else None
                pv(kk, r)
                r = rn
            for m in range(2):
                rz, drz = ep[f"rz{m}"]
                kb.op(kb.dve, lambda: nc.vector.reciprocal(out=rz, in_=ctx.ps[Zb[m]]), reads=[ctx.psd[Zb[m]]], writes=[drz])
                o, do = ep[f"o{m + 1}"]
                kb.op(kb.dve, lambda: nc.vector.tensor_tensor(out=o, in0=ctx.ps[Ob[m]], in1=rz, op=ALU.mult), reads=[ctx.psd[Ob[m]], drz], writes=[do])
            oo, doo = ep["oo"]
            kb.op(kb.dve, lambda: nc.vector.scalar_tensor_tensor(out=oo, in0=ep["o2"][0], scalar=nlam[:, 0:1], in1=ep["o1"][0], op0=ALU.mult, op1=ALU.add),
                  reads=[ep["o2"][1], ep["o1"][1], d_nlam], writes=[doo])
            sq, dsq = ep["sq"]
            kb.op(kb.pool, lambda: nc.gpsimd.tensor_tensor(out=sq, in0=oo, in1=oo, op=ALU.mult), reads=[doo], writes=[dsq])
            b = Sb[0][sbi[0] % 2]; sbi[0] += 1
            psr, dpsr = ctx.ps[b], ctx.psd[b]
            kb.op(kb.pe, lambda: nc.tensor.matmul(psr, lhsT=ones_f, rhs=sq, start=True, stop=True), reads=[d_ones_f, dsq], writes=[dpsr])
            rt, drt = ep["rt"]
            kb.op(kb.act, lambda: nc.scalar.activation(out=rt, in_=psr, func=AF.Sqrt, bias=epsc[:, 0:1], scale=1.0 / 128), reads=[dpsr, d_epsc], writes=[drt])
            rr, drr = ep["rr"]
            kb.op(kb.dve, lambda: nc.vector.reciprocal(out=rr, in_=rt), reads=[drt], writes=[drr])
            ob, dob = obp.next()
            kb.op(kb.dve, lambda: nc.vector.scalar_tensor_tensor(out=ob, in0=oo, scalar=gsc[:, 0:1], in1=rr, op0=ALU.mult, op1=ALU.mult),
                  reads=[doo, drr, d_gsc], writes=[dob])
            if "_oa_dst" in D:
                oa_ap, oa_dep = D["_oa_dst"](h, g)
            else:
                oa_ap, oa_dep = D["oa"][h, :, g * 512:(g + 1) * 512], d_oa
            kb.dma(kb.pool, oa_ap, ob, reads=[dob], writes=[oa_dep], sd=dob)
    return [d_oa]


def consts_A(r):
    import ml_dtypes
    bf = ml_dtypes.bfloat16
    qrows = np.zeros((2, 3, 512), np.float32)
    krows = np.zeros((2, 3, S), np.float32)
    cdiag = np.zeros((2, 128, 128), np.float32)
    offtab = np.zeros((2, 128, 67), np.float32)
    qr = np.arange(512)
    kpos = np.arange(S) % 128
    kk = np.arange(128)[:, None]; qq = np.arange(128)[None, :]
    for i in range(2):
        s = SLOPES[2 * r + i]
        qrows[i, 0] = -s * ((qr // 256) * 256)
        qrows[i, 1] = -s * (qr % 256)
        qrows[i, 2] = 1.0
        krows[i, 0] = 1.0
        krows[i, 1] = 1.0
        krows[i, 2] = s * kpos
        c = np.where(kk <= qq, 0.0, np.where(kk // 64 == qq // 64, -2.0 * s * (kk - qq), NEG))
        cdiag[i] = c
        offtab[i] = (-s * 128.0 * (np.arange(67) - 3))[None, :]
    return {"qrows": qrows.astype(bf), "krows": krows.astype(bf), "cdiag": cdiag, "offtab": offtab,
            "ones_b": np.ones((128, 128), bf), "ones_f": np.ones((128, 128), np.float32),
            "ident": np.eye(128, dtype=np.float32).astype(bf)}


def build_L1():
    nc = bass.Bass("TRN2", target_bir_lowering=False)
    D = {}
    def din(name, shape, dt):
        D[name] = nc.dram_tensor(name, list(shape), dt, kind="ExternalInput").ap()
    din("xb", [S, 1024], F32); din("wa", [1024, 768], F32); din("lamv", [1, 256], F32); din("subg", [128, 1], F32)
    din("qrows", [2, 3, 512], BF16); din("krows", [2, 3, S], BF16); din("cdiag", [2, 128, 128], F32); din("offtab", [2, 128, 67], F32)
    din("ones_b", [128, 128], BF16); din("ones_f", [128, 128], F32); din("ident", [128, 128], BF16)
    D["oa"] = nc.dram_tensor("oa", [2, 128, S], BF16, kind="ExternalOutput").ap()
    D["QT"] = nc.dram_tensor("QT", [2, 128, S], BF16, kind="Internal").ap()
    D["KT"] = nc.dram_tensor("KT", [2, 128, S], BF16, kind="Internal").ap()
    D["V"] = nc.dram_tensor("V", [S, 256], BF16, kind="Internal").ap()
    with nc.cleanup_on_exit():
        kb = KB(nc)
        ctx = Ctx(nc, kb, D["ident"])
        outs = phase_A(nc, kb, ctx, D)
        for E in (kb.pool, kb.sp):
            kb.wait_all(E, outs)
        nc.all_engine_barrier()
    print("L1 built: ins", kb.nins, "waits", kb.nwait, "sems", kb.nsem)
    return nc


def inputs_L1(inp, c):
    b, r = c // 2, c % 2
    w = inp["w_in_even"][0]
    hs = [2 * r, 2 * r + 1]
    cols = [w[:, h * 128:(h + 1) * 128] for h in hs] + [w[:, 512 + h * 128: 512 + (h + 1) * 128] for h in hs] + \
           [w[:, 1024 + h * 128: 1024 + (h + 1) * 128] for h in hs]
    d = {"xb": np.ascontiguousarray(inp["x"][b]), "wa": np.ascontiguousarray(np.concatenate(cols, axis=1)),
         "lamv": np.concatenate([inp["lam_q1"][0], inp["lam_k1"][0], inp["lam_q2"][0], inp["lam_k2"][0]])[None, :].astype(np.float32),
         "subg": np.ascontiguousarray(inp["diff_subln_g"][0].reshape(128, 1))}
    d.update(consts_A(r))
    return d


ALPHA = 4.0 ** 0.25
T = 4096
GAM = [1.0 - 2.0 ** (-5.0 - h) for h in range(4)]


def ctile(nc, kb, name, shape, dt, src, q=None):
    t = nc.alloc_sbuf_tensor(name + "_sb", list(shape), dt).ap(); d = Dep(name)
    kb.dma(q or kb.sp, t, src, writes=[d], sd=d)
    return t, d


def bcast_row(ap_row, n):
    return ap_row.partition_broadcast(128).rearrange("p o n -> p (o n)")


class LN:
    def __init__(self, nc, kb, name, g_row, b_row):
        self.nc, self.kb = nc, kb
        self.gB, self.d_gB = ctile(nc, kb, name + "gB", [128, 1024], F32, bcast_row(g_row, 1024))
        self.bB, self.d_bB = ctile(nc, kb, name + "bB", [128, 1024], F32, bcast_row(b_row, 1024))
        self.yp = Pool(nc, name + "y", [128, 1024], F32, 2)
        self.xnp = Pool(nc, name + "xn", [128, 1024], F32, 2)
        self.stp = Pool(nc, name + "st", [128, 2, 6], F32, 2)
        self.mvp = Pool(nc, name + "mv", [128, 2], F32, 2)
        self.rsp = Pool(nc, name + "rs", [128, 1], F32, 2)
        self.eps = nc.alloc_sbuf_tensor(name + "eps", [128, 1], F32).ap(); self.d_eps = Dep(name + "eps")
        kb.op(kb.dve, lambda: nc.vector.memset(self.eps, 1e-5), writes=[self.d_eps])

    def apply(self, x_t, d_x, banks, out_t, d_out):
        nc, kb = self.nc, self.kb
        y, dy = self.yp.next()
        for hf in range(2):
            ps, dps = banks[hf]
            kb.op(kb.dve, lambda: nc.vector.scalar_tensor_tensor(out=y[:, hf * 512:(hf + 1) * 512], in0=x_t[:, hf * 512:(hf + 1) * 512], scalar=ALPHA,
                                                                 in1=ps, op0=ALU.mult, op1=ALU.add),
                  reads=[d_x, dps], writes=[dy], acc=True)
        st, dst = self.stp.next()
        for hf in range(2):
            kb.op(kb.dve, lambda: nc.vector.bn_stats(out=st[:, hf, :], in_=y[:, hf * 512:(hf + 1) * 512]), reads=[dy], writes=[dst], acc=True)
        mv, dmv = self.mvp.next()
        kb.op(kb.dve, lambda: nc.vector.bn_aggr(out=mv, in_=st.rearrange("p a b -> p (a b)")), reads=[dst], writes=[dmv])
        rs, drs = self.rsp.next()
        kb.op(kb.act, lambda: nc.scalar.activation(out=rs, in_=mv[:, 1:2], func=AF.Sqrt, bias=self.eps[:, 0:1]), reads=[dmv, self.d_eps], writes=[drs])
        kb.op(kb.dve, lambda: nc.vector.reciprocal(out=rs, in_=rs), reads=[drs], writes=[drs])
        xn, dxn = self.xnp.next()
        kb.op(kb.dve, lambda: nc.vector.tensor_scalar(out=xn, in0=y, scalar1=mv[:, 0:1], scalar2=rs[:, 0:1], op0=ALU.subtract, op1=ALU.mult),
              reads=[dy, dmv, drs], writes=[dxn])
        kb.op(kb.pool, lambda: nc.gpsimd.tensor_tensor(out=xn, in0=xn, in1=self.gB, op=ALU.mult), reads=[dxn, self.d_gB], writes=[dxn])
        kb.op(kb.pool, lambda: nc.gpsimd.tensor_tensor(out=out_t, in0=xn, in1=self.bB, op=ALU.add), reads=[dxn, self.d_bB], writes=[d_out])


class XOut:
    def __init__(self, nc, kb, ctx, name, x_dram, d_xd, xT_dram, d_xTd, tbanks):
        self.nc, self.kb, self.ctx = nc, kb, ctx
        self.x_dram, self.d_xd, self.xT_dram, self.d_xTd, self.tbanks = x_dram, d_xd, xT_dram, d_xTd, tbanks
        self.op_ = Pool(nc, name + "xo", [128, 1024], F32, 2)
        if xT_dram is not None:
            self.bp = Pool(nc, name + "xob", [128, 1024], BF16, 2)
            self.tp = Pool(nc, name + "xoT", [128, 8, 128], BF16, 2)

    def next_tile(self):
        return self.op_.next()

    def store(self, xo, dxo, tok0):
        nc, kb = self.nc, self.kb
        kb.dma(kb.pool, self.x_dram[tok0:tok0 + 128, :], xo, reads=[dxo], writes=[self.d_xd], sd=dxo)
        if self.xT_dram is not None:
            xb, dxb = self.bp.next()
            kb.op(kb.act, lambda: nc.scalar.copy(out=xb, in_=xo), reads=[dxo], writes=[dxb])
            xT, dxT = self.tp.next()
            self.ctx.transpose_into(xb, dxb, xT, dxT, 8, self.tbanks)
            kb.dma(kb.pool, self.xT_dram[:, :, tok0:tok0 + 128], xT, reads=[dxT], writes=[self.d_xTd], sd=dxT)


def phase_B(nc, kb, ctx, D, dd):
    kb.phase_begin()
    NT = 36
    VB = nc.alloc_sbuf_tensor("VB", [128, NT, 8, 66], BF16).ap(); d_VB = Dep("VB")
    mark2 = (nc.sbuf_base, nc.sbuf_top)
    wB = nc.alloc_sbuf_tensor("wB", [128, 8, 1536], BF16).ap(); d_wB = Dep("wB")
    wst = Pool(nc, "wstB", [128, 8, 256], F32, 2)
    load_w_bf16(nc, kb, D["wb"], wB, d_wB, 1536, wst, 256)
    kb.op(kb.pool, lambda: nc.gpsimd.memset(VB.rearrange("p a b c -> p (a b c)"), 1.0), writes=[d_VB])
    xTp = Pool(nc, "xTB", [128, 8, 512], BF16, 2)
    fst = Pool(nc, "fstB", [128, 512], BF16, 3)
    d_QbT = Dep("QbT", multi=True); d_KbT = Dep("KbT", multi=True)
    fmb = [2, 3]; vb = [4, 5]; fi = 0; vi = 0
    for g in range(NT // 4):
        xT, d_xT = xTp.next()
        ctx.make_xT(D["xh"][g * 512:(g + 1) * 512, :], xT, d_xT, [0, 1])
        for ci in range(8):
            if g == 0 and ci < 4:
                continue
            b = fmb[fi % 2]; fi += 1
            ps, dps = ctx.ps[b], ctx.psd[b]
            for k in range(8):
                kb.op(kb.pe, lambda: nc.tensor.matmul(ps, lhsT=wB[:, k, ci * 128:(ci + 1) * 128], rhs=xT[:, k, :], start=(k == 0), stop=(k == 7)),
                      reads=[d_wB, d_xT], writes=[dps], acc=True)
            st, dst = fst.next()
            kb.op(kb.act, lambda: nc.scalar.copy(out=st, in_=ps), reads=[dps], writes=[dst])
            dstd, ddd = (D["QbT"], d_QbT) if ci < 4 else (D["KbT"], d_KbT)
            kb.dma(kb.pool, dstd[ci % 4, :, g * 512:(g + 1) * 512], st, reads=[dst], writes=[ddd], sd=dst)
        for t in range(4):
            b = vb[vi % 2]; vi += 1
            ps, dps = ctx.ps[b], ctx.psd[b]
            for k in range(8):
                kb.op(kb.pe, lambda: nc.tensor.matmul(ps, lhsT=xT[:, k, t * 128:(t + 1) * 128], rhs=wB[:, k, 1024:1536], start=(k == 0), stop=(k == 7)),
                      reads=[d_wB, d_xT], writes=[dps], acc=True)
            kb.op(kb.dve, lambda: nc.vector.tensor_copy(out=VB[:, g * 4 + t, :, 0:64], in_=ps.rearrange("p (h e) -> p h e", h=8)),
                  reads=[dps], writes=[d_VB], acc=True)
    hv, d_hv = ctile(nc, kb, "hv", [128, 1], F32, D["hv"])
    vh = VB[:, 0:4].rearrange("p a b c -> p (a b c)")
    kb.op(kb.dve, lambda: nc.vector.tensor_scalar(out=vh, in0=vh, scalar1=hv[:, 0:1], scalar2=None, op0=ALU.mult), reads=[d_VB, d_hv], writes=[d_VB])
    kb.phase_end(mark2)

    tb = nc.alloc_sbuf_tensor("tb_sb", [128, 8, 2, 128], F32).ap(); d_tb = Dep("tb")
    for h in range(8):
        kb.dma(kb.sp, tb[:, h], D["tb"][h].rearrange("j k q -> k j q"), writes=[d_tb], sd=d_tb)
    cb, d_cb = ctile(nc, kb, "cb", [128, 8], F32, D["cb"])
    M0, d_M0 = ctile(nc, kb, "m0", [128, 512], BF16, D["m0"])
    m4, d_m4 = ctile(nc, kb, "m4", [128, 512], BF16, D["m4"])
    kb.op(kb.dve, lambda: nc.vector.tensor_scalar(out=cb, in0=cb, scalar1=-1.0, scalar2=None, op0=ALU.mult), reads=[d_cb], writes=[d_cb])
    M3 = []; M4 = []
    for hq in range(2):
        a = nc.alloc_sbuf_tensor(f"M3_{hq}", [128, 512], BF16).ap(); da = Dep(f"M3_{hq}")
        b_ = nc.alloc_sbuf_tensor(f"M4_{hq}", [128, 512], BF16).ap(); db = Dep(f"M4_{hq}")
        for hi in range(4):
            h = hq * 4 + hi
            kb.op(kb.act, lambda: nc.scalar.activation(out=a[:, hi * 128:(hi + 1) * 128], in_=tb[:, h, 0, :], func=AF.Exp, bias=cb[:, h:h + 1]),
                  reads=[d_tb, d_cb], writes=[da], acc=True)
            kb.op(kb.act, lambda: nc.scalar.activation(out=b_[:, hi * 128:(hi + 1) * 128], in_=tb[:, h, 1, :], func=AF.Exp, bias=cb[:, h:h + 1]),
                  reads=[d_tb, d_cb], writes=[db], acc=True)
        kb.op(kb.dve, lambda: nc.vector.tensor_tensor(out=b_, in0=b_, in1=m4, op=ALU.mult), reads=[db, d_m4], writes=[db])
        M3.append((a, da)); M4.append((b_, db))

    wo = nc.alloc_sbuf_tensor("wo0", [128, 8, 1024], BF16).ap(); d_wo = Dep("wo0")
    wst2 = Pool(nc, "wstB2", [128, 8, 128], F32, 2)
    load_w_bf16(nc, kb, D["wout0"], wo, d_wo, 1024, wst2, 128)
    ln = LN(nc, kb, "lnm0", D["lnp"][0:1, :], D["lnp"][1:2, :])
    xo = XOut(nc, kb, ctx, "m0", D["x1"], dd["x1"], D["x1T"], dd["x1T"], [6])
    Kp = Pool(nc, "KbTt", [128, 4, 128], BF16, 8)
    Qp = Pool(nc, "QbTt", [128, 4, 128], BF16, 2)
    pTp = Pool(nc, "pTB", [128, 512], BF16, 3)
    obp = Pool(nc, "obB", [128, 512], BF16, 2)
    obTp = Pool(nc, "obTB", [128, 4, 128], BF16, 2)
    oap = Pool(nc, "oaTB", [128, 4, 128], BF16, 2)
    xrp = Pool(nc, "xresB", [128, 1024], F32, 2)
    rzp = Pool(nc, "rzB", [128, 4], F32, 2)
    Sb = [0, 1]; si = 0
    Obk = [2, 3]; oi = 0
    ktiles = {}
    mk2, d_mk2 = ctile(nc, kb, "mk2", [128, 2], F32, D["mk2"])
    if "oag_0" in D:
        selv, d_selv = ctile(nc, kb, "selv", [128, 2], F32, D["selv"])
        oap0 = Pool(nc, "oaTB0", [128, 4, 128], BF16, 2)
        oap1 = Pool(nc, "oaTB1", [128, 4, 128], BF16, 2)
    qzp = [Pool(nc, f"qzB{i}", [128, 4, 128], BF16, 2) for i in range(2)]

    def load_k(lt):
        kt, dk = Kp.next()
        kb.dma(kb.sp, kt, D["KbT"][:, :, lt * 128:(lt + 1) * 128].rearrange("c p t -> p c t"), reads=[d_KbT], writes=[dk], sd=dk)
        ktiles[lt] = (kt, dk)
    for lt in range(4):
        load_k(lt)
    for qb in range(32):
        load_k(qb + 4)
        qt, dq = Qp.next()
        kb.dma(kb.sp, qt, D["QbT"][:, :, (qb + 4) * 128:(qb + 5) * 128].rearrange("c p t -> p c t"), reads=[d_QbT], writes=[dq], sd=dq)
        ob, dob = obp.next()
        qz = []
        for i in range(2):
            z, dz = qzp[i].next()
            kb.op(kb.dve, lambda: nc.vector.tensor_scalar(out=z, in0=qt, scalar1=mk2[:, i:i + 1], scalar2=None, op0=ALU.mult), reads=[dq, d_mk2], writes=[dz])
            qz.append((z, dz))
        for hq in range(2):
            ob_b = Obk[oi % 2]; oi += 1
            pso, dpso = ctx.ps[ob_b], ctx.psd[ob_b]

            def qk(j):
                nonlocal si
                b = Sb[si % 2]; si += 1
                ps, dps = ctx.ps[b], ctx.psd[b]
                kt, dk = ktiles[qb + j]
                for hi in range(4):
                    h = hq * 4 + hi; c = h // 2; hf = h % 2
                    kb.op(kb.pe, lambda: nc.tensor.matmul(ps[:, hi * 128:(hi + 1) * 128], lhsT=kt[:, c, :], rhs=qz[hf][0][:, c, :],
                                                          start=True, stop=True),
                          reads=[dk, qz[hf][1]], writes=[dps], acc=True)
                pT, dpT = pTp.next()
                kb.op(kb.act, lambda: nc.scalar.activation(out=pT, in_=ps, func=AF.Exp, scale=0.125), reads=[dps], writes=[dpT])
                mt = {0: (M0, d_M0), 3: M3[hq], 4: M4[hq]}.get(j)
                if mt is not None:
                    kb.op(kb.dve, lambda: nc.vector.tensor_tensor(out=pT, in0=pT, in1=mt[0], op=ALU.mult), reads=[dpT, mt[1]], writes=[dpT])
                return pT, dpT

            def pv(j, r):
                pT, dpT = r
                for hi in range(4):
                    h = hq * 4 + hi
                    kb.op(kb.pe, lambda: nc.tensor.matmul(pso[:, hi * 66:(hi + 1) * 66], lhsT=pT[:, hi * 128:(hi + 1) * 128], rhs=VB[:, qb + j, h, :],
                                                          start=(j == 0 and hi == 0), stop=(j == 4 and hi == 3)),
                          reads=[dpT, d_VB], writes=[dpso], acc=True)
            r = qk(0)
            for j in range(5):
                rn = qk(j + 1) if j < 4 else None
                pv(j, r)
                r = rn
            rz, drz = rzp.next()
            pv3 = pso[:, 0:264].rearrange("p (h e) -> p h e", h=4)
            kb.op(kb.dve, lambda: nc.vector.reciprocal(out=rz, in_=pv3[:, :, 64]), reads=[dpso], writes=[drz])
            kb.op(kb.dve, lambda: nc.vector.tensor_tensor(out=ob[:, hq * 256:(hq + 1) * 256].rearrange("p (h e) -> p h e", h=4), in0=pv3[:, :, 0:64],
                                                          in1=rz.unsqueeze(2).broadcast_to([128, 4, 64]), op=ALU.mult),
                  reads=[dpso, drz], writes=[dob], acc=True)
        obT, dobT = obTp.next()
        ctx.transpose_into(ob, dob, obT, dobT, 4, [6])
        oat, doa = oap.next()
        if "oag_0" in D:
            o0, do0 = oap0.next(); o1, do1 = oap1.next()
            t_ = qb * 128
            k0, k1, col = t_ // 1024, 4 + t_ // 1024, t_ % 1024
            og0 = D[f"oag_{k0}"].rearrange("(c p) t -> p c t", c=4)
            og1 = D[f"oag_{k1}"].rearrange("(c p) t -> p c t", c=4)
            kb.dma(kb.sp, o0, og0[:, :, col:col + 128], reads=[dd[f"oag_{k0}"]], writes=[do0], sd=do0)
            kb.dma(kb.sp, o1, og1[:, :, col:col + 128], reads=[dd[f"oag_{k1}"]], writes=[do1], sd=do1)
            kb.op(kb.dve, lambda: nc.vector.tensor_scalar(out=oat, in0=o0, scalar1=selv[:, 0:1], scalar2=None, op0=ALU.mult), reads=[do0, d_selv], writes=[doa])
            kb.op(kb.dve, lambda: nc.vector.scalar_tensor_tensor(out=oat, in0=o1, scalar=selv[:, 1:2], in1=oat, op0=ALU.mult, op1=ALU.add),
                  reads=[do1, d_selv, doa], writes=[doa])
        else:
            kb.dma(kb.sp, oat, D["oaT"][:, :, qb * 128:(qb + 1) * 128].rearrange("c p t -> p c t"), writes=[doa], sd=doa)
        xr, dxr = xrp.next()
        kb.dma(kb.sp, xr, D["xh"][512 + qb * 128: 512 + (qb + 1) * 128, :], writes=[dxr], sd=dxr)
        banks = []
        for hf in range(2):
            b = [4, 5][hf]
            ps, dps = ctx.ps[b], ctx.psd[b]
            for c in range(8):
                lt_, dl = (oat[:, c, :], doa) if c < 4 else (obT[:, c - 4, :], dobT)
                kb.op(kb.pe, lambda: nc.tensor.matmul(ps, lhsT=lt_, rhs=wo[:, c, hf * 512:(hf + 1) * 512], start=(c == 0), stop=(c == 7)),
                      reads=[dl, d_wo], writes=[dps], acc=True)
            banks.append((ps, dps))
        xt_, dxt_ = xo.next_tile()
        ln.apply(xr, dxr, banks, xt_, dxt_)
        xo.store(xt_, dxt_, qb * 128)
    kb.phase_end()


def phase_Fup(nc, kb, ctx, D, dd, wg_d, wu_d, xT_name, tag):
    kb.phase_begin()
    wg = nc.alloc_sbuf_tensor(tag + "wg", [128, 8, 2816], BF16).ap(); d_wg = Dep(tag + "wg")
    wu = nc.alloc_sbuf_tensor(tag + "wu", [128, 8, 2816], BF16).ap(); d_wu = Dep(tag + "wu")
    wst = Pool(nc, tag + "wstF", [128, 8, 352], F32, 2)
    load_w_bf16(nc, kb, wg_d, wg, d_wg, 2816, wst, 352)
    load_w_bf16(nc, kb, wu_d, wu, d_wu, 2816, wst, 352)
    xTp = Pool(nc, tag + "xTF", [128, 8, 512], BF16, 2)
    sgp = Pool(nc, tag + "sgF", [128, 512], F32, 2)
    hp = Pool(nc, tag + "hF", [128, 512], BF16, 3)
    gi = 0
    for g in range(T // 512):
        xT, d_xT = xTp.next()
        kb.dma(kb.sp, xT, D[xT_name][:, :, g * 512:(g + 1) * 512], reads=[dd[xT_name]], writes=[d_xT], sd=d_xT)
        for f in range(22):
            bg = [0, 1][gi % 2]; bu = [2, 3][gi % 2]; gi += 1
            for (w_, dw_, b) in ((wg, d_wg, bg), (wu, d_wu, bu)):
                ps, dps = ctx.ps[b], ctx.psd[b]
                for k in range(8):
                    kb.op(kb.pe, lambda: nc.tensor.matmul(ps, lhsT=w_[:, k, f * 128:(f + 1) * 128], rhs=xT[:, k, :], start=(k == 0), stop=(k == 7)),
                          reads=[dw_, d_xT], writes=[dps], acc=True)
            sg, dsg = sgp.next()
            kb.op(kb.act, lambda: nc.scalar.activation(out=sg, in_=ctx.ps[bg], func=AF.Silu), reads=[ctx.psd[bg]], writes=[dsg])
            ht, dh = hp.next()
            kb.op(kb.dve, lambda: nc.vector.tensor_tensor(out=ht, in0=sg, in1=ctx.ps[bu], op=ALU.mult), reads=[dsg, ctx.psd[bu]], writes=[dh])
            kb.dma(kb.pool, D["hT"][f, :, g * 512:(g + 1) * 512], ht, reads=[dh], writes=[dd["hT"]], sd=dh)
    kb.phase_end()


def phase_Fdown(nc, kb, ctx, D, dd, wd_d, x_name, g_row, b_row, out_name, outT_name, tag):
    kb.phase_begin()
    wd = nc.alloc_sbuf_tensor(tag + "wd", [128, 22, 1024], BF16).ap(); d_wd = Dep(tag + "wd")
    wst = Pool(nc, tag + "wstD", [128, 22, 128], F32, 2)
    load_w_bf16(nc, kb, wd_d, wd, d_wd, 1024, wst, 128)
    ln = LN(nc, kb, tag + "lnf", g_row, b_row)
    xo = XOut(nc, kb, ctx, tag + "fd", D[out_name], dd[out_name], D[outT_name] if outT_name else None, dd[outT_name] if outT_name else None, [6, 7])
    hp = Pool(nc, tag + "hD", [128, 22, 512], BF16, 2)
    xrp = Pool(nc, tag + "xresD", [128, 1024], F32, 2)
    bi = 0
    for g in range(T // 512):
        hg, dhg = hp.next()
        kb.dma(kb.sp, hg, D["hT"][:, :, g * 512:(g + 1) * 512].rearrange("f p t -> p f t"), reads=[dd["hT"]], writes=[dhg], sd=dhg)
        for t in range(4):
            tok0 = g * 512 + t * 128
            xr, dxr = xrp.next()
            kb.dma(kb.sp, xr, D[x_name][tok0:tok0 + 128, :], reads=[dd[x_name]], writes=[dxr], sd=dxr)
            banks = []
            for hf in range(2):
                b = [[0, 1], [2, 3]][bi % 2][hf]
                ps, dps = ctx.ps[b], ctx.psd[b]
                for f in range(22):
                    kb.op(kb.pe, lambda: nc.tensor.matmul(ps, lhsT=hg[:, f, t * 128:(t + 1) * 128], rhs=wd[:, f, hf * 512:(hf + 1) * 512], start=(f == 0), stop=(f == 21)),
                          reads=[dhg, d_wd], writes=[dps], acc=True)
                banks.append((ps, dps))
            bi += 1
            xt_, dxt_ = xo.next_tile()
            ln.apply(xr, dxr, banks, xt_, dxt_)
            xo.store(xt_, dxt_, tok0)
    kb.phase_end()


def phase_P1(nc, kb, ctx, D, dd):
    kb.phase_begin()
    w1 = nc.alloc_sbuf_tensor("w1_sb", [128, 8, 2560], BF16).ap(); d_w1 = Dep("w1")
    wst = Pool(nc, "wstP", [128, 8, 256], F32, 2)
    load_w_bf16(nc, kb, D["w1"], w1, d_w1, 2560, wst, 256)
    wsf, d_wsf = ctile(nc, kb, "wsf", [128, 4, 128], F32, D["sguwT"])
    trl, d_trl = ctile(nc, kb, "trl", [128, 128], F32, D["trilT"])
    wsT = nc.alloc_sbuf_tensor("wsT", [128, 4, 128], BF16).ap(); d_wsT = Dep("wsT")
    kb.op(kb.dve, lambda: nc.vector.tensor_tensor(out=wsT, in0=wsf, in1=trl.unsqueeze(1).broadcast_to([128, 4, 128]), op=ALU.mult),
          reads=[d_wsf, d_trl], writes=[d_wsT])
    sbb, d_sbb = ctile(nc, kb, "sgub", [128, 4], F32, D["sgub"])
    lgB, d_lgB = ctile(nc, kb, "sgB", [128, 512], F32, bcast_row(D["sgulng"], 512))
    lbB, d_lbB = ctile(nc, kb, "sbB", [128, 512], F32, bcast_row(D["sgulnb"], 512))
    eps = nc.alloc_sbuf_tensor("epsP", [128, 1], F32).ap(); d_eps = Dep("epsP")
    kb.op(kb.dve, lambda: nc.vector.memset(eps, 1e-5), writes=[d_eps])
    xTp = Pool(nc, "xTP", [128, 8, 512], BF16, 2)
    fst = Pool(nc, "fstP", [128, 512], BF16, 3)
    kst = Pool(nc, "kstP", [128, 256], BF16, 2)
    vst = Pool(nc, "vstP", [128, 512], BF16, 2)
    gst = Pool(nc, "gstP", [128, 512], BF16, 2)
    up = Pool(nc, "uP", [128, 512], F32, 2)
    vp = Pool(nc, "vP", [128, 512], F32, 2)
    vnp = Pool(nc, "vnP", [128, 512], BF16, 2)
    odp = Pool(nc, "odP", [128, 512], BF16, 2)
    stp = Pool(nc, "stP", [128, 6], F32, 2)
    mvp = Pool(nc, "mvP", [128, 2], F32, 2)
    rsp = Pool(nc, "rsP", [128, 1], F32, 2)
    tmb = [0, 1, 2, 3]; ti = 0
    fmb = [5, 6]; fi = 0

    def tm_mm(xT, d_xT, t, c0, c1):
        nonlocal ti
        b = tmb[ti % 4]; ti += 1
        ps, dps = ctx.ps[b], ctx.psd[b]
        n = c1 - c0
        for k in range(8):
            kb.op(kb.pe, lambda: nc.tensor.matmul(ps[:, 0:n], lhsT=xT[:, k, t * 128:(t + 1) * 128], rhs=w1[:, k, c0:c1], start=(k == 0), stop=(k == 7)),
                  reads=[d_w1, d_xT], writes=[dps], acc=True)
        return ps, dps

    for g in range(T // 512):
        xT, d_xT = xTp.next()
        kb.dma(kb.sp, xT, D["x2T"][:, :, g * 512:(g + 1) * 512], reads=[dd["x2T"]], writes=[d_xT], sd=d_xT)
        for ci in range(4):
            b = fmb[fi % 2]; fi += 1
            ps, dps = ctx.ps[b], ctx.psd[b]
            for k in range(8):
                kb.op(kb.pe, lambda: nc.tensor.matmul(ps, lhsT=w1[:, k, ci * 128:(ci + 1) * 128], rhs=xT[:, k, :], start=(k == 0), stop=(k == 7)),
                      reads=[d_w1, d_xT], writes=[dps], acc=True)
            st, dst = fst.next()
            kb.op(kb.act, lambda: nc.scalar.copy(out=st, in_=ps), reads=[dps], writes=[dst])
            nm = "qcT" if ci < 2 else "kcT"
            kb.dma(kb.pool, D[nm][ci % 2, :, g * 512:(g + 1) * 512], st, reads=[dst], writes=[dd[nm]], sd=dst)
        for t in range(4):
            tok0 = g * 512 + t * 128
            ps, dps = tm_mm(xT, d_xT, t, 256, 512)
            st, dst = kst.next()
            kb.op(kb.dve, lambda: nc.vector.tensor_copy(out=st, in_=ps[:, 0:256]), reads=[dps], writes=[dst])
            kb.dma(kb.pool, D["kc"][tok0:tok0 + 128, :], st, reads=[dst], writes=[dd["kc"]], sd=dst)
            ps, dps = tm_mm(xT, d_xT, t, 512, 1024)
            st, dst = vst.next()
            kb.op(kb.dve, lambda: nc.vector.tensor_copy(out=st, in_=ps), reads=[dps], writes=[dst])
            kb.dma(kb.pool, D["vc"][tok0:tok0 + 128, :], st, reads=[dst], writes=[dd["vc"]], sd=dst)
            ps, dps = tm_mm(xT, d_xT, t, 1024, 1536)
            st, dst = gst.next()
            kb.op(kb.act, lambda: nc.scalar.activation(out=st, in_=ps, func=AF.Silu), reads=[dps], writes=[dst])
            kb.dma(kb.pool, D["sg"][tok0:tok0 + 128, :], st, reads=[dst], writes=[dd["sg"]], sd=dst)
            ps, dps = tm_mm(xT, d_xT, t, 1536, 2048)
            u, du = up.next()
            kb.op(kb.act, lambda: nc.scalar.activation(out=u, in_=ps, func=AF.Gelu_apprx_tanh), reads=[dps], writes=[du])
            ps, dps = tm_mm(xT, d_xT, t, 2048, 2560)
            v, dv = vp.next()
            kb.op(kb.act, lambda: nc.scalar.activation(out=v, in_=ps, func=AF.Gelu_apprx_tanh), reads=[dps], writes=[dv])
            st6, dst6 = stp.next()
            kb.op(kb.dve, lambda: nc.vector.bn_stats(out=st6, in_=v), reads=[dv], writes=[dst6])
            mv, dmv = mvp.next()
            kb.op(kb.dve, lambda: nc.vector.bn_aggr(out=mv, in_=st6), reads=[dst6], writes=[dmv])
            rs, drs = rsp.next()
            kb.op(kb.act, lambda: nc.scalar.activation(out=rs, in_=mv[:, 1:2], func=AF.Sqrt, bias=eps[:, 0:1]), reads=[dmv, d_eps], writes=[drs])
            kb.op(kb.dve, lambda: nc.vector.reciprocal(out=rs, in_=rs), reads=[drs], writes=[drs])
            kb.op(kb.dve, lambda: nc.vector.tensor_scalar(out=v, in0=v, scalar1=mv[:, 0:1], scalar2=rs[:, 0:1], op0=ALU.subtract, op1=ALU.mult),
                  reads=[dv, dmv, drs], writes=[dv])
            kb.op(kb.pool, lambda: nc.gpsimd.tensor_tensor(out=v, in0=v, in1=lgB, op=ALU.mult), reads=[dv, d_lgB], writes=[dv])
            vn, dvn = vnp.next()
            kb.op(kb.pool, lambda: nc.gpsimd.tensor_tensor(out=vn, in0=v, in1=lbB, op=ALU.add), reads=[dv, d_lbB], writes=[dvn])
            psm, dpsm = ctx.ps[4], ctx.psd[4]
            for gg in range(4):
                kb.op(kb.pe, lambda: nc.tensor.matmul(psm[:, gg * 128:(gg + 1) * 128], lhsT=wsT[:, gg, :], rhs=vn[:, gg * 128:(gg + 1) * 128], start=True, stop=True),
                      reads=[d_wsT, dvn], writes=[dpsm], acc=True)
            od, dod = odp.next()
            for gg in range(4):
                kb.op(kb.dve, lambda: nc.vector.scalar_tensor_tensor(out=od[:, gg * 128:(gg + 1) * 128], in0=psm[:, gg * 128:(gg + 1) * 128], scalar=sbb[:, gg:gg + 1],
                                                                     in1=u[:, gg * 128:(gg + 1) * 128], op0=ALU.add, op1=ALU.mult),
                      reads=[dpsm, d_sbb, du], writes=[dod], acc=True)
            kb.dma(kb.pool, D["od"][tok0:tok0 + 128, :], od, reads=[dod], writes=[dd["od"]], sd=dod)
    kb.phase_end()


def ret_consts():
    m = np.arange(128)[:, None]; l = np.arange(128)[None, :]
    DT = np.zeros((128, 4, 128), np.float32)
    qdec = np.zeros((128, 2, 128), np.float32)
    kdec = np.zeros((128, 4), np.float32)
    g128 = np.zeros((128, 2), np.float32)
    for h in range(4):
        g = GAM[h]
        DT[:, h, :] = np.where(l >= m, g ** np.maximum(l - m, 0).astype(np.float64), 0.0) / 8.0
        kdec[:, h] = g ** (127.0 - np.arange(128))
        c2, hf = h // 2, h % 2
        qdec[hf * 64:(hf + 1) * 64, c2, :] = (g ** (np.arange(128) + 1.0))[None, :] / 8.0
        g128[hf * 64:(hf + 1) * 64, c2] = g ** 128.0
    qdecm = np.zeros((128, 4, 2, 128), np.float32)
    for hf in range(2):
        sl = slice(hf * 64, (hf + 1) * 64)
        qdecm[sl, hf, :, :] = 1.0
        qdecm[sl, 2 + hf, :, :] = qdec[sl]
    mk2 = np.zeros((128, 2), np.float32); mk2[0:64, 0] = 1.0; mk2[64:128, 1] = 1.0
    return {"DT": DT.reshape(128, 512), "qdec": qdec, "qdecm": qdecm, "kdec": kdec, "g128": g128, "mk2": mk2}


def ret_state_update(nc, kb, ctx, kct, dkc, vct, dvc, kdec, d_kdec, g128, d_g128, Sf, d_Sf, kdp, bank):
    kd, dkd = kdp.next()
    kb.op(kb.pool, lambda: nc.gpsimd.tensor_tensor(out=kd.rearrange("p (h d) -> p h d", h=4), in0=kct.rearrange("p (h d) -> p h d", h=4),
                                                   in1=kdec.unsqueeze(2).broadcast_to([128, 4, 64]), op=ALU.mult),
          reads=[dkc, d_kdec], writes=[dkd])
    ps, dps = ctx.ps[bank], ctx.psd[bank]
    for h in range(4):
        c2, hf = h // 2, h % 2
        kb.op(kb.pe, lambda: nc.tensor.matmul(ps[:, h * 128:(h + 1) * 128], lhsT=kd[:, c2 * 128:(c2 + 1) * 128], rhs=vct[:, h * 128:(h + 1) * 128],
                                              start=True, stop=True),
              reads=[dkd, dvc], writes=[dps], acc=True)
    for h in range(4):
        c2, hf = h // 2, h % 2
        sl = slice(hf * 64, (hf + 1) * 64)
        kb.op(kb.dve, lambda: nc.vector.scalar_tensor_tensor(out=Sf[sl, c2, :], in0=Sf[sl, c2, :], scalar=g128[sl, c2:c2 + 1], in1=ps[sl, h * 128:(h + 1) * 128],
                                                             op0=ALU.mult, op1=ALU.add),
              reads=[d_Sf, dps, d_g128], writes=[d_Sf], acc=True)


def phase_C1(nc, kb, ctx, D, dd):
    kb.phase_begin()
    kdec, d_kdec = ctile(nc, kb, "kdec1", [128, 4], F32, D["kdec"])
    g128, d_g128 = ctile(nc, kb, "g1281", [128, 2], F32, D["g128"])
    Sf = nc.alloc_sbuf_tensor("Sf1", [128, 2, 128], F32).ap(); d_Sf = Dep("Sf1")
    kb.op(kb.dve, lambda: nc.vector.memset(Sf.rearrange("p a b -> p (a b)"), 0.0), writes=[d_Sf])
    kcp = Pool(nc, "kc1", [128, 256], BF16, 3)
    vcp = Pool(nc, "vc1", [128, 512], BF16, 3)
    kdp = Pool(nc, "kd1", [128, 256], BF16, 2)
    for c in range(T // 128):
        kct, dkc = kcp.next()
        kb.dma(kb.sp, kct, D["kc"][c * 128:(c + 1) * 128, :], reads=[dd["kc"]], writes=[dkc], sd=dkc)
        vct, dvc = vcp.next()
        kb.dma(kb.sp, vct, D["vc"][c * 128:(c + 1) * 128, :], reads=[dd["vc"]], writes=[dvc], sd=dvc)
        ret_state_update(nc, kb, ctx, kct, dkc, vct, dvc, kdec, d_kdec, g128, d_g128, Sf, d_Sf, kdp, c % 2)
    lst_v = D["lst"] if len(D["lst"].shape) == 3 else D["lst"].rearrange("p (a b) -> p a b", a=2)
    kb.dma(kb.pool, lst_v, Sf, reads=[d_Sf], writes=[dd["lst"]], sd=d_Sf)
    kb.phase_end()


def phase_C2(nc, kb, ctx, D, dd):
    kb.phase_begin()
    kdec, d_kdec = ctile(nc, kb, "kdec2", [128, 4], F32, D["kdec"])
    g128, d_g128 = ctile(nc, kb, "g1282", [128, 2], F32, D["g128"])
    DT, d_DT = ctile(nc, kb, "DT", [128, 512], F32, D["DT"])
    qdec, d_qdec = ctile(nc, kb, "qdec", [128, 4, 2, 128], F32, D["qdecm"])
    gng, d_gng = ctile(nc, kb, "gng", [128, 512], F32, bcast_row(D["gng"], 512))
    gnb, d_gnb = ctile(nc, kb, "gnb", [128, 512], F32, bcast_row(D["gnb"], 512))
    if "lstg" in D:
        Sf = nc.alloc_sbuf_tensor("Sf2_sb", [128, 2, 128], F32).ap(); d_Sf = Dep("Sf2")
        kb.dma(kb.sp, Sf, D["lstg"][0:128, :].rearrange("p (a b) -> p a b", a=2), reads=[dd["lstg"]], writes=[d_Sf], sd=d_Sf)
        hv2, d_hv2 = ctile(nc, kb, "hv2", [128, 1], F32, D["hv"])
        sfl = Sf.rearrange("p a b -> p (a b)")
        kb.op(kb.dve, lambda: nc.vector.tensor_scalar(out=sfl, in0=sfl, scalar1=hv2[:, 0:1], scalar2=None, op0=ALU.mult), reads=[d_Sf, d_hv2], writes=[d_Sf])
    else:
        Sf, d_Sf = ctile(nc, kb, "Sf2", [128, 2, 128], F32, D["sinit"])
    Sb_ = nc.alloc_sbuf_tensor("Sb2", [128, 2, 128], BF16).ap(); d_Sb = Dep("Sb2")
    kb.op(kb.act, lambda: nc.scalar.copy(out=Sb_, in_=Sf), reads=[d_Sf], writes=[d_Sb])
    eps = nc.alloc_sbuf_tensor("epsC", [128, 1], F32).ap(); d_eps = Dep("epsC")
    kb.op(kb.dve, lambda: nc.vector.memset(eps, 1e-5), writes=[d_eps])
    wo = nc.alloc_sbuf_tensor("wo1", [128, 8, 1024], BF16).ap(); d_wo = Dep("wo1")
    wst = Pool(nc, "wstC", [128, 8, 256], F32, 2)
    load_w_bf16(nc, kb, D["wout1"], wo, d_wo, 1024, wst, 256)
    ln = LN(nc, kb, "lnm1", D["lnp"][4:5, :], D["lnp"][5:6, :])
    xo = XOut(nc, kb, ctx, "m1", D["x3"], dd["x3"], D["x3T"], dd["x3T"], [7])
    qTp = Pool(nc, "qT2", [128, 2, 128], BF16, 2)
    kTp = Pool(nc, "kT2", [128, 2, 128], BF16, 2)
    kcp = Pool(nc, "kc2", [128, 256], BF16, 2)
    vcp = Pool(nc, "vc2", [128, 512], BF16, 2)
    sgp = Pool(nc, "sg2", [128, 512], BF16, 2)
    mxp = Pool(nc, "mx2", [128, 1024], BF16, 2)
    kdp = Pool(nc, "kd2", [128, 256], BF16, 2)
    wTp = Pool(nc, "wT2", [128, 512], BF16, 2)
    qdp = Pool(nc, "qd2", [128, 4, 2, 128], BF16, 2)
    onp = Pool(nc, "on2", [128, 512], F32, 2)
    stp = Pool(nc, "st2", [128, 4, 6], F32, 2)
    mvp = Pool(nc, "mv2", [128, 4, 2], F32, 2)
    rsp = Pool(nc, "rs2", [128, 4], F32, 2)
    mTp = Pool(nc, "mT2", [128, 8, 128], BF16, 2)
    xrp = Pool(nc, "xres2", [128, 1024], F32, 2)
    for c in range(T // 128):
        tok0 = c * 128
        qT, dqT = qTp.next()
        kb.dma(kb.sp, qT, D["qcT"][:, :, tok0:tok0 + 128].rearrange("c p t -> p c t"), reads=[dd["qcT"]], writes=[dqT], sd=dqT)
        kT, dkT = kTp.next()
        kb.dma(kb.sp, kT, D["kcT"][:, :, tok0:tok0 + 128].rearrange("c p t -> p c t"), reads=[dd["kcT"]], writes=[dkT], sd=dkT)
        kct, dkc = kcp.next()
        kb.dma(kb.sp, kct, D["kc"][tok0:tok0 + 128, :], reads=[dd["kc"]], writes=[dkc], sd=dkc)
        vct, dvc = vcp.next()
        kb.dma(kb.sp, vct, D["vc"][tok0:tok0 + 128, :], reads=[dd["vc"]], writes=[dvc], sd=dvc)
        sgt, dsg = sgp.next()
        kb.dma(kb.sp, sgt, D["sg"][tok0:tok0 + 128, :], reads=[dd["sg"]], writes=[dsg], sd=dsg)
        mx, dmx = mxp.next()
        kb.dma(kb.sp, mx[:, 512:1024], D["od"][tok0:tok0 + 128, :], reads=[dd["od"]], writes=[dmx], sd=dmx)
        xr, dxr = xrp.next()
        kb.dma(kb.sp, xr, D["x2"][tok0:tok0 + 128, :], reads=[dd["x2"]], writes=[dxr], sd=dxr)
        qd, dqd = qdp.next()
        for v4 in range(4):
            kb.op(kb.pool, lambda: nc.gpsimd.tensor_tensor(out=qd[:, v4], in0=qT, in1=qdec[:, v4], op=ALU.mult), reads=[dqT, d_qdec], writes=[dqd], acc=True)
        pss, dpss = ctx.ps[0], ctx.psd[0]
        for h in range(4):
            c2, hf = h // 2, h % 2
            kb.op(kb.pe, lambda: nc.tensor.matmul(pss[:, h * 128:(h + 1) * 128], lhsT=kT[:, c2, :], rhs=qd[:, hf, c2, :], start=True, stop=True),
                  reads=[dkT, dqd], writes=[dpss], acc=True)
        wT, dwT = wTp.next()
        kb.op(kb.dve, lambda: nc.vector.tensor_tensor(out=wT, in0=pss, in1=DT, op=ALU.mult), reads=[dpss, d_DT], writes=[dwT])
        pso, dpso = ctx.ps[1], ctx.psd[1]
        for h in range(4):
            c2, hf = h // 2, h % 2
            kb.op(kb.pe, lambda: nc.tensor.matmul(pso[:, h * 128:(h + 1) * 128], lhsT=wT[:, h * 128:(h + 1) * 128], rhs=vct[:, h * 128:(h + 1) * 128], start=(h == 0), stop=False),
                  reads=[dwT, dvc], writes=[dpso], acc=True)
            kb.op(kb.pe, lambda: nc.tensor.matmul(pso[:, h * 128:(h + 1) * 128], lhsT=qd[:, 2 + hf, c2, :], rhs=Sb_[:, c2, :], start=False, stop=(h == 3)),
                  reads=[dqd, d_Sb], writes=[dpso], acc=True)
        ret_state_update(nc, kb, ctx, kct, dkc, vct, dvc, kdec, d_kdec, g128, d_g128, Sf, d_Sf, kdp, 2)
        kb.op(kb.act, lambda: nc.scalar.copy(out=Sb_, in_=Sf), reads=[d_Sf], writes=[d_Sb])
        st, dst = stp.next()
        for h in range(4):
            kb.op(kb.dve, lambda: nc.vector.bn_stats(out=st[:, h, :], in_=pso[:, h * 128:(h + 1) * 128]), reads=[dpso], writes=[dst], acc=True)
        mv, dmv = mvp.next()
        for h in range(4):
            kb.op(kb.dve, lambda: nc.vector.bn_aggr(out=mv[:, h, :], in_=st[:, h, :]), reads=[dst], writes=[dmv], acc=True)
        rs, drs = rsp.next()
        kb.op(kb.act, lambda: nc.scalar.activation(out=rs, in_=mv[:, :, 1], func=AF.Sqrt, bias=eps[:, 0:1]), reads=[dmv, d_eps], writes=[drs])
        kb.op(kb.dve, lambda: nc.vector.reciprocal(out=rs, in_=rs), reads=[drs], writes=[drs])
        on, don = onp.next()
        for h in range(4):
            kb.op(kb.dve, lambda: nc.vector.tensor_scalar(out=on[:, h * 128:(h + 1) * 128], in0=pso[:, h * 128:(h + 1) * 128], scalar1=mv[:, h, 0:1], scalar2=rs[:, h:h + 1],
                                                          op0=ALU.subtract, op1=ALU.mult),
                  reads=[dpso, dmv, drs], writes=[don], acc=True)
        kb.op(kb.pool, lambda: nc.gpsimd.tensor_tensor(out=on, in0=on, in1=gng, op=ALU.mult), reads=[don, d_gng], writes=[don])
        kb.op(kb.pool, lambda: nc.gpsimd.tensor_tensor(out=on, in0=on, in1=gnb, op=ALU.add), reads=[don, d_gnb], writes=[don])
        kb.op(kb.pool, lambda: nc.gpsimd.tensor_tensor(out=mx[:, 0:512], in0=on, in1=sgt, op=ALU.mult), reads=[don, dsg], writes=[dmx], acc=True)
        mT, dmT = mTp.next()
        ctx.transpose_into(mx, dmx, mT, dmT, 8, [3])
        banks = []
        for hf in range(2):
            b = [4, 5][hf]
            ps, dps = ctx.ps[b], ctx.psd[b]
            for cc in range(8):
                kb.op(kb.pe, lambda: nc.tensor.matmul(ps, lhsT=mT[:, cc, :], rhs=wo[:, cc, hf * 512:(hf + 1) * 512], start=(cc == 0), stop=(cc == 7)),
                      reads=[dmT, d_wo], writes=[dps], acc=True)
            banks.append((ps, dps))
        xt_, dxt_ = xo.next_tile()
        ln.apply(xr, dxr, banks, xt_, dxt_)
        xo.store(xt_, dxt_, tok0)
    kb.phase_end()

from concourse.bass_utils import run_bass_kernel_spmd
import ml_dtypes

BFNP = ml_dtypes.bfloat16


def _mk(nc, D, dd, name, shape, dt, kind):
    D[name] = nc.dram_tensor(name, list(shape), dt, kind=kind).ap()
    dd[name] = Dep(name, multi=True)


def build_L2(phases='BUDPC'):
    nc = bass.Bass("TRN2", target_bir_lowering=False)
    D, dd = {}, {}
    I, O, N = "ExternalInput", "ExternalOutput", "Internal"
    for name, shape, dt in [("xh", [4608, 1024], F32), ("wb", [1024, 1536], F32), ("oaT", [4, 128, T], BF16), ("wout0", [1024, 1024], F32),
                            ("lnp", [8, 1024], F32), ("wg", [1024, 2816], F32), ("wu", [1024, 2816], F32), ("wd", [2816, 1024], F32),
                            ("w1", [1024, 2560], F32), ("sguwT", [128, 4, 128], F32), ("trilT", [128, 128], F32), ("sgub", [128, 4], F32),
                            ("sgulng", [1, 512], F32), ("sgulnb", [1, 512], F32), ("tb", [8, 2, 128, 128], F32), ("cb", [128, 8], F32),
                            ("m0", [128, 512], BF16), ("m4", [128, 512], BF16), ("hv", [128, 1], F32), ("ident", [128, 128], BF16),
                            ("kdec", [128, 4], F32), ("g128", [128, 2], F32), ("mk2", [128, 2], F32)]:
        _mk(nc, D, dd, name, shape, dt, I)
    for name, shape, dt in [("x2", [T, 1024], F32), ("qcT", [2, 128, T], BF16), ("kcT", [2, 128, T], BF16), ("kc", [T, 256], BF16),
                            ("vc", [T, 512], BF16), ("sg", [T, 512], BF16), ("od", [T, 512], BF16), ("lst", [128, 2, 128], F32), ("x1", [T, 1024], F32)]:
        _mk(nc, D, dd, name, shape, dt, O)
    for name, shape, dt in [("QbT", [4, 128, 4608], BF16), ("KbT", [4, 128, 4608], BF16), ("x1T", [128, 8, T], BF16),
                            ("hT", [22, 128, T], BF16), ("x2T", [128, 8, T], BF16)]:
        _mk(nc, D, dd, name, shape, dt, N)
    with nc.cleanup_on_exit():
        kb = KB(nc)
        ctx = Ctx(nc, kb, D["ident"])
        if 'B' in phases:
            phase_B(nc, kb, ctx, D, dd)
        if 'U' in phases:
            phase_Fup(nc, kb, ctx, D, dd, D["wg"], D["wu"], "x1T", "f0")
        if 'D' in phases:
            phase_Fdown(nc, kb, ctx, D, dd, D["wd"], "x1", D["lnp"][2:3, :], D["lnp"][3:4, :], "x2", "x2T", "f0")
        if 'P' in phases:
            phase_P1(nc, kb, ctx, D, dd)
        if 'C' in phases:
            phase_C1(nc, kb, ctx, D, dd)
        outs = [dd[n] for n in ("x2", "qcT", "kcT", "kc", "vc", "sg", "od", "lst")]
        for E in (kb.pool, kb.sp):
            kb.wait_all(E, outs)
        nc.all_engine_barrier()
    print("L2 built: ins", kb.nins, "waits", kb.nwait, "sems", kb.nsem)
    return nc


def build_L3():
    nc = bass.Bass("TRN2", target_bir_lowering=False)
    D, dd = {}, {}
    I, O, N = "ExternalInput", "ExternalOutput", "Internal"
    for name, shape, dt in [("x2", [T, 1024], F32), ("qcT", [2, 128, T], BF16), ("kcT", [2, 128, T], BF16), ("kc", [T, 256], BF16),
                            ("vc", [T, 512], BF16), ("sg", [T, 512], BF16), ("od", [T, 512], BF16), ("sinit", [128, 2, 128], F32),
                            ("wout1", [1024, 1024], F32), ("lnp", [8, 1024], F32), ("wg", [1024, 2816], F32), ("wu", [1024, 2816], F32),
                            ("wd", [2816, 1024], F32), ("gng", [1, 512], F32), ("gnb", [1, 512], F32), ("DT", [128, 512], F32),
                            ("qdecm", [128, 4, 2, 128], F32), ("kdec", [128, 4], F32), ("g128", [128, 2], F32), ("ident", [128, 128], BF16)]:
        _mk(nc, D, dd, name, shape, dt, I)
    _mk(nc, D, dd, "out", [T, 1024], F32, O)
    for name, shape, dt in [("x3", [T, 1024], F32), ("x3T", [128, 8, T], BF16), ("hT", [22, 128, T], BF16)]:
        _mk(nc, D, dd, name, shape, dt, N)
    with nc.cleanup_on_exit():
        kb = KB(nc)
        ctx = Ctx(nc, kb, D["ident"])
        phase_C2(nc, kb, ctx, D, dd)
        phase_Fup(nc, kb, ctx, D, dd, D["wg"], D["wu"], "x3T", "f1")
        phase_Fdown(nc, kb, ctx, D, dd, D["wd"], "x3", D["lnp"][6:7, :], D["lnp"][7:8, :], "out", None, "f1")
        for E in (kb.pool, kb.sp):
            kb.wait_all(E, [dd["out"]])
        nc.all_engine_barrier()
    print("L3 built: ins", kb.nins, "waits", kb.nwait, "sems", kb.nsem)
    return nc


def inputs_L2(inp, c, oa_all):
    b, r = c // 2, c % 2
    x = inp["x"]
    xh = np.zeros((4608, 1024), np.float32)
    if r == 1:
        xh[0:512] = x[b, 4096 - 512:4096]
    xh[512:] = x[b, 4096 * r:4096 * (r + 1)]
    oaT = None
    if oa_all is not None:
        oaT = np.stack([oa_all[2 * b + a // 2][a % 2][:, 4096 * r:4096 * (r + 1)] for a in range(4)]).astype(BFNP)
    rb = inp["rel_bias"][0]
    kk = np.arange(128)[:, None]; qq = np.arange(128)[None, :]
    i3 = np.minimum(256 + qq - kk, 256); i4 = 128 + qq - kk
    tb = np.stack([np.stack([rb[h][i3], rb[h][i4]]) for h in range(8)]).astype(np.float32)
    cb = np.ascontiguousarray(np.broadcast_to(rb[:, 256][None, :], (128, 8))).astype(np.float32)
    m0 = np.where((kk < 64) & (qq >= 64), 0.0, 1.0); m4 = np.where((kk >= 64) & (qq < 64), 0.0, 1.0)
    rc = ret_consts()
    lnp = np.stack([inp["ln_mix_g"][0], inp["ln_mix_b"][0], inp["ln_ffn_g"][0], inp["ln_ffn_b"][0],
                    inp["ln_mix_g"][1], inp["ln_mix_b"][1], inp["ln_ffn_g"][1], inp["ln_ffn_b"][1]]).astype(np.float32)
    return {"xh": xh, "wb": np.ascontiguousarray(inp["w_in_even"][0][:, 1536:3072]), "oaT": oaT,
            "wout0": np.ascontiguousarray(inp["w_out_even"][0]), "lnp": lnp,
            "wg": np.ascontiguousarray(inp["ffn_w_gate"][0]), "wu": np.ascontiguousarray(inp["ffn_w_up"][0]),
            "wd": np.ascontiguousarray(inp["ffn_w_down"][0]), "w1": np.ascontiguousarray(inp["w_in_odd"][0]),
            "sguwT": np.ascontiguousarray(inp["sgu_w"][0].transpose(2, 0, 1)), "trilT": (kk <= qq).astype(np.float32),
            "sgub": np.ascontiguousarray(inp["sgu_b"][0].T), "sgulng": inp["sgu_ln_g"][0][None, :].astype(np.float32),
            "sgulnb": inp["sgu_ln_b"][0][None, :].astype(np.float32), "tb": tb, "cb": cb,
            "m0": np.tile(m0, (1, 4)).astype(BFNP), "m4": np.tile(m4, (1, 4)).astype(BFNP),
            "hv": np.full((128, 1), float(r), np.float32), "ident": np.eye(128, dtype=np.float32).astype(BFNP),
            "kdec": rc["kdec"], "g128": rc["g128"], "mk2": rc["mk2"]}, lnp


def inputs_L3(inp, c, r2, lnp):
    b, r = c // 2, c % 2
    rc = ret_consts()
    me = r2[c]
    sinit = np.asarray(r2[c - 1]["lst"]) if r == 1 else np.zeros((128, 2, 128), np.float32)
    d = {k: np.asarray(me[k]) for k in ("x2", "qcT", "kcT", "kc", "vc", "sg", "od")}
    d.update({"sinit": sinit.astype(np.float32), "wout1": np.ascontiguousarray(inp["w_out_odd"][0]), "lnp": lnp,
              "wg": np.ascontiguousarray(inp["ffn_w_gate"][1]), "wu": np.ascontiguousarray(inp["ffn_w_up"][1]),
              "wd": np.ascontiguousarray(inp["ffn_w_down"][1]), "gng": inp["ret_gn_g"][0][None, :].astype(np.float32),
              "gnb": inp["ret_gn_b"][0][None, :].astype(np.float32), "DT": rc["DT"], "qdecm": rc["qdecm"], "kdec": rc["kdec"],
              "g128": rc["g128"], "ident": np.eye(128, dtype=np.float32).astype(BFNP)})
    return d


_CACHE = {}


def _prog(name, fn):
    if name not in _CACHE:
        _CACHE[name] = fn()
    return _CACHE[name]


def kernel(**inp):
    inp = {k: np.asarray(v) for k, v in inp.items()}
    cores = list(range(8))
    r1 = run_bass_kernel_spmd(_prog("L1", build_L1), [inputs_L1(inp, c) for c in cores], core_ids=cores).results
    oa_all = [np.asarray(r1[c]["oa"]) for c in cores]
    in2 = [inputs_L2(inp, c, oa_all) for c in cores]
    lnp = in2[0][1]
    r2 = run_bass_kernel_spmd(_prog("L2", build_L2), [i[0] for i in in2], core_ids=cores).results
    in3 = [inputs_L3(inp, c, r2, lnp) for c in cores]
    r3 = run_bass_kernel_spmd(_prog("L3", build_L3), in3, core_ids=cores).results
    out = np.zeros((4, 8192, 1024), np.float32)
    for c in cores:
        b, r = c // 2, c % 2
        out[b, 4096 * r:4096 * (r + 1)] = np.asarray(r3[c]["out"])
    return out


PAIRS = [[0, 1], [2, 3], [4, 5], [6, 7]]


def build_fused(cc=3):
    nc = bass.Bass("TRN2", target_bir_lowering=False)
    D, dd = {}, {}
    I, O, N = "ExternalInput", "ExternalOutput", "Internal"
    for name, shape, dt in [("xb", [8192, 1024], F32), ("wa", [1024, 768], F32), ("lamv", [1, 256], F32), ("subg", [128, 1], F32),
                            ("qrows", [2, 3, 512], BF16), ("krows", [2, 3, 8192], BF16), ("cdiag", [2, 128, 128], F32), ("offtab", [2, 128, 67], F32),
                            ("ones_b", [128, 128], BF16), ("ones_f", [128, 128], F32), ("ident", [128, 128], BF16),
                            ("xh", [4608, 1024], F32), ("wb", [1024, 1536], F32), ("wout0", [1024, 1024], F32),
                            ("lnp", [8, 1024], F32), ("wg0", [1024, 2816], F32), ("wu0", [1024, 2816], F32), ("wd0", [2816, 1024], F32),
                            ("w1", [1024, 2560], F32), ("sguwT", [128, 4, 128], F32), ("trilT", [128, 128], F32), ("sgub", [128, 4], F32),
                            ("sgulng", [1, 512], F32), ("sgulnb", [1, 512], F32), ("tb", [8, 2, 128, 128], F32), ("cb", [128, 8], F32),
                            ("m0", [128, 512], BF16), ("m4", [128, 512], BF16), ("hv", [128, 1], F32),
                            ("kdec", [128, 4], F32), ("g128", [128, 2], F32), ("mk2", [128, 2], F32), ("selv", [128, 2], F32),
                            ("wout1", [1024, 1024], F32), ("wg1", [1024, 2816], F32), ("wu1", [1024, 2816], F32), ("wd1", [2816, 1024], F32),
                            ("gng", [1, 512], F32), ("gnb", [1, 512], F32), ("DT", [128, 512], F32), ("qdecm", [128, 4, 2, 128], F32)]:
        _mk(nc, D, dd, name, shape, dt, I)
    _mk(nc, D, dd, "out", [T, 1024], F32, O)
    for k in range(8):
        _mk(nc, D, dd, f"oa2_{k}", [256, 1024], BF16, N)
        _mk(nc, D, dd, f"oag_{k}", [512, 1024], BF16, N)
    for name, shape, dt in [("QT", [2, 128, 8192], BF16), ("KT", [2, 128, 8192], BF16),
                            ("V", [8192, 256], BF16), ("QbT", [4, 128, 4608], BF16), ("KbT", [4, 128, 4608], BF16), ("x1", [T, 1024], F32),
                            ("x1T", [128, 8, T], BF16), ("hT", [22, 128, T], BF16), ("x2", [T, 1024], F32), ("x2T", [128, 8, T], BF16),
                            ("qcT", [2, 128, T], BF16), ("kcT", [2, 128, T], BF16), ("kc", [T, 256], BF16), ("vc", [T, 512], BF16),
                            ("sg", [T, 512], BF16), ("od", [T, 512], BF16), ("lst", [128, 256], F32), ("lstg", [256, 256], F32),
                            ("x3", [T, 1024], F32), ("x3T", [128, 8, T], BF16)]:
        _mk(nc, D, dd, name, shape, dt, N)
    D["_oa_dst"] = lambda h, g: (D[f"oa2_{g // 2}"][h * 128:(h + 1) * 128, (g % 2) * 512:(g % 2 + 1) * 512], dd[f"oa2_{g // 2}"])
    with nc.cleanup_on_exit():
        kb = KB(nc)
        ctx = Ctx(nc, kb, D["ident"])
        kb.phase_begin()
        phase_A(nc, kb, ctx, D)
        if cc & 1:
            for k in range(8):
                kb.collective("AllGather", PAIRS, D[f"oa2_{k}"], D[f"oag_{k}"], reads=[dd[f"oa2_{k}"]], writes=[dd[f"oag_{k}"]])
        kb.phase_end()
        phase_B(nc, kb, ctx, D, dd)
        phase_Fup(nc, kb, ctx, D, dd, D["wg0"], D["wu0"], "x1T", "f0")
        phase_Fdown(nc, kb, ctx, D, dd, D["wd0"], "x1", D["lnp"][2:3, :], D["lnp"][3:4, :], "x2", "x2T", "f0")
        phase_P1(nc, kb, ctx, D, dd)
        phase_C1(nc, kb, ctx, D, dd)
        if cc & 2:
            kb.collective("AllGather", PAIRS, D["lst"], D["lstg"], reads=[dd["lst"]], writes=[dd["lstg"]])
        phase_C2(nc, kb, ctx, D, dd)
        phase_Fup(nc, kb, ctx, D, dd, D["wg1"], D["wu1"], "x3T", "f1")
        phase_Fdown(nc, kb, ctx, D, dd, D["wd1"], "x3", D["lnp"][6:7, :], D["lnp"][7:8, :], "out", None, "f1")
        for E in (kb.pool, kb.sp):
            kb.wait_all(E, [dd["out"]])
        nc.all_engine_barrier()
    print("fused built: ins", kb.nins, "waits", kb.nwait, "sems", kb.nsem)
    return nc


def inputs_fused(inp, c):
    b, r = c // 2, c % 2
    d = inputs_L1(inp, c)
    d2, lnp = inputs_L2(inp, c, None)
    for k in ("wg", "wu", "wd"):
        d2[k + "0"] = d2.pop(k)
    d2.pop("oaT", None)
    d.update(d2)
    rc = ret_consts()
    sel = np.zeros((128, 2), np.float32); sel[:, r] = 1.0
    d.update({"selv": sel, "wout1": np.ascontiguousarray(inp["w_out_odd"][0]),
              "wg1": np.ascontiguousarray(inp["ffn_w_gate"][1]), "wu1": np.ascontiguousarray(inp["ffn_w_up"][1]),
              "wd1": np.ascontiguousarray(inp["ffn_w_down"][1]), "gng": inp["ret_gn_g"][0][None, :].astype(np.float32),
              "gnb": inp["ret_gn_b"][0][None, :].astype(np.float32), "DT": rc["DT"], "qdecm": rc["qdecm"]})
    return d


def kernel(**inp):
    inp = {k: np.asarray(v) for k, v in inp.items()}
    cores = list(range(8))
    res = run_bass_kernel_spmd(_prog("F", build_fused), [inputs_fused(inp, c) for c in cores], core_ids=cores).results
    out = np.zeros((4, 8192, 1024), np.float32)
    for c in cores:
        b, r = c // 2, c % 2
        out[b, 4096 * r:4096 * (r + 1)] = np.asarray(res[c]["out"])
    return out
```
